# Optimizing a Trainium2 kernel written in Bass

```python
import jax, jax.numpy as jnp
from jax import lax
import numpy as np

D_MODEL = 1024
BATCH = 2
SEQ = 8192
DEPTH = 1

LRU_WIDTH = D_MODEL
LRU_HEADS = 16
LRU_BLOCK = LRU_WIDTH // LRU_HEADS
CONV_WIDTH = 4
CONV_LEFT = 2
RGLRU_C = 8.0
N_DIR = 2
N_HEADS = 16
N_KV_HEADS = 4
HEAD_DIM = 64
GROUP = N_HEADS // N_KV_HEADS
WINDOW = 128
BLOCK = 128
D_FF = ((8 * D_MODEL // 3 + 255) // 256) * 256
N_BRANCH = 2
Q_W = N_HEADS * HEAD_DIM
KV_W = N_KV_HEADS * HEAD_DIM
IN_W = 2 * LRU_WIDTH + Q_W + 2 * KV_W + N_BRANCH * D_MODEL
EPS = 1e-6
NEG_INF = -1e30

kernel_name = "hybrid_rglru_swa_gated_encoder"


def rmsnorm(x, g):
    xf = x.astype(jnp.float32)
    y = xf * lax.rsqrt(jnp.mean(xf * xf, axis=-1, keepdims=True) + EPS)
    return (y * g.astype(jnp.float32)).astype(x.dtype)


def centred_depthwise_conv(u, w, b):
    s = u.shape[1]
    up = jnp.pad(u, ((0, 0), (CONV_LEFT, CONV_WIDTH - 1 - CONV_LEFT), (0, 0)))
    out = up[:, 0:s] * w[0]
    for k in range(1, CONV_WIDTH):
        out = out + up[:, k:k + s] * w[k]
    return out + b


def _linear_combine(left, right):
    a1, b1 = left
    a2, b2 = right
    return a1 * a2, a2 * b1 + b2


def rg_lru(u, lam, wa, ba, wx, bx, reverse):
    bsz, s, c = u.shape
    ub = u.reshape(bsz, s, LRU_HEADS, LRU_BLOCK)
    r = jax.nn.sigmoid(jnp.einsum("bshi,hij->bshj", ub, wa.astype(jnp.float32)).reshape(bsz, s, c) + ba.astype(jnp.float32))
    i = jax.nn.sigmoid(jnp.einsum("bshi,hij->bshj", ub, wx.astype(jnp.float32)).reshape(bsz, s, c) + bx.astype(jnp.float32))
    log_a = -RGLRU_C * r * jax.nn.softplus(-lam.astype(jnp.float32))
    a = jnp.exp(log_a)
    beta = jnp.sqrt(jnp.maximum(-jnp.expm1(2.0 * log_a), 0.0))
    _, h = lax.associative_scan(_linear_combine, (a, beta * (i * u)), axis=1, reverse=reverse)
    return h


def banded_alibi_sink_attention(q, k, v, sink):
    bsz, s = q.shape[0], q.shape[1]
    nb = s // BLOCK
    qb = (q.astype(jnp.float32) * (HEAD_DIM ** -0.5)).reshape(bsz, nb, BLOCK, N_KV_HEADS, GROUP, HEAD_DIM)

    def key_blocks(t):
        tp = jnp.pad(t.astype(jnp.float32), ((0, 0), (BLOCK, BLOCK), (0, 0), (0, 0)))
        tp = tp.reshape(bsz, nb + 2, BLOCK, N_KV_HEADS, HEAD_DIM)
        return jnp.concatenate([tp[:, j:j + nb] for j in range(3)], axis=2)

    kb = key_blocks(k)
    vb = key_blocks(v)
    scores = jnp.einsum("bnqkgd,bnskd->bnkgqs", qb, kb)

    q_loc = jnp.arange(BLOCK)
    k_loc = jnp.arange(3 * BLOCK)
    dist = q_loc[:, None] + BLOCK - k_loc[None, :]
    kpos = jnp.arange(nb)[:, None] * BLOCK - BLOCK + k_loc[None, :]
    valid = (jnp.abs(dist) <= WINDOW)[None] & ((kpos >= 0) & (kpos < s))[:, None, :]

    slopes = jnp.exp2(-8.0 * (jnp.arange(N_HEADS, dtype=jnp.float32) + 1.0) / N_HEADS)
    alibi = -slopes.reshape(N_KV_HEADS, GROUP, 1, 1) * jnp.abs(dist).astype(jnp.float32)
    scores = jnp.where(valid[None, :, None, None], scores + alibi, NEG_INF)

    sink_l = sink.astype(jnp.float32).reshape(1, 1, N_KV_HEADS, GROUP, 1, 1)
    m = jnp.maximum(jnp.max(scores, axis=-1, keepdims=True), sink_l)
    p = jnp.exp(scores - m)
    denom = jnp.sum(p, axis=-1, keepdims=True) + jnp.exp(sink_l - m)
    o = jnp.einsum("bnkgqs,bnskd->bnqkgd", p / denom, vb)
    return o.reshape(bsz, s, Q_W)


def setup_inputs(seed: int = 0) -> dict:
    key = jax.random.key(seed)
    ks = jax.random.split(key, 20)
    f32 = jnp.float32
    x = jax.random.normal(ks[0], (BATCH, SEQ, D_MODEL), f32)
    norm_mix_g = 1.0 + 0.05 * jax.random.normal(ks[1], (DEPTH, D_MODEL), f32)
    w_in = jax.random.normal(ks[2], (DEPTH, D_MODEL, IN_W), f32) * D_MODEL ** -0.5
    b_gate = 0.01 * jax.random.normal(ks[3], (DEPTH, N_BRANCH * D_MODEL), f32)
    conv_w = jax.random.normal(ks[4], (DEPTH, CONV_WIDTH, LRU_WIDTH), f32) * CONV_WIDTH ** -0.5
    conv_b = 0.01 * jax.random.normal(ks[5], (DEPTH, LRU_WIDTH), f32)
    u = jax.random.uniform(ks[6], (DEPTH, N_DIR, LRU_WIDTH), f32, minval=0.9, maxval=0.999)
    p = u ** (1.0 / RGLRU_C)
    lru_lambda = jnp.log(p) - jnp.log1p(-p)
    lru_wa = jax.random.normal(ks[7], (DEPTH, N_DIR, LRU_HEADS, LRU_BLOCK, LRU_BLOCK), f32) * LRU_BLOCK ** -0.5
    lru_ba = 0.01 * jax.random.normal(ks[8], (DEPTH, N_DIR, LRU_WIDTH), f32)
    lru_wx = jax.random.normal(ks[9], (DEPTH, N_DIR, LRU_HEADS, LRU_BLOCK, LRU_BLOCK), f32) * LRU_BLOCK ** -0.5
    lru_bx = 0.01 * jax.random.normal(ks[10], (DEPTH, N_DIR, LRU_WIDTH), f32)
    attn_sink = 0.5 * jax.random.normal(ks[11], (DEPTH, N_HEADS), f32)
    w_out = jax.random.normal(ks[12], (DEPTH, D_MODEL, D_MODEL), f32) * D_MODEL ** -0.5
    norm_ffn_g = 1.0 + 0.05 * jax.random.normal(ks[13], (DEPTH, D_MODEL), f32)
    w_ffn_in = jax.random.normal(ks[14], (DEPTH, D_MODEL, 2 * D_FF), f32) * D_MODEL ** -0.5
    w_ffn_out = jax.random.normal(ks[15], (DEPTH, D_FF, D_MODEL), f32) * D_FF ** -0.5
    norm_final_g = 1.0 + 0.05 * jax.random.normal(ks[16], (D_MODEL,), f32)
    return {"x": x, "norm_mix_g": norm_mix_g, "w_in": w_in, "b_gate": b_gate,
            "conv_w": conv_w, "conv_b": conv_b, "lru_lambda": lru_lambda,
            "lru_wa": lru_wa, "lru_ba": lru_ba, "lru_wx": lru_wx, "lru_bx": lru_bx,
            "attn_sink": attn_sink, "w_out": w_out, "norm_ffn_g": norm_ffn_g,
            "w_ffn_in": w_ffn_in, "w_ffn_out": w_ffn_out, "norm_final_g": norm_final_g}


def reference(x, norm_mix_g, w_in, b_gate, conv_w, conv_b, lru_lambda, lru_wa, lru_ba,
              lru_wx, lru_bx, attn_sink, w_out, norm_ffn_g, w_ffn_in, w_ffn_out, norm_final_g):
    bsz, s, _ = x.shape
    splits = [LRU_WIDTH, 2 * LRU_WIDTH, 2 * LRU_WIDTH + Q_W, 2 * LRU_WIDTH + Q_W + KV_W,
              2 * LRU_WIDTH + Q_W + 2 * KV_W]
    for l in range(DEPTH):
        xn = rmsnorm(x, norm_mix_g[l])
        proj = xn @ w_in[l]
        u, g_lru, q, k, v, z = jnp.split(proj, splits, axis=-1)

        uc = centred_depthwise_conv(u, conv_w[l], conv_b[l]).astype(jnp.float32)
        h_fwd = rg_lru(uc, lru_lambda[l, 0], lru_wa[l, 0], lru_ba[l, 0], lru_wx[l, 0], lru_bx[l, 0], False)
        h_bwd = rg_lru(uc, lru_lambda[l, 1], lru_wa[l, 1], lru_ba[l, 1], lru_wx[l, 1], lru_bx[l, 1], True)
        y_a = ((h_fwd + h_bwd) * jax.nn.gelu(g_lru.astype(jnp.float32))).astype(x.dtype)

        y_b = banded_alibi_sink_attention(
            q.reshape(bsz, s, N_HEADS, HEAD_DIM),
            k.reshape(bsz, s, N_KV_HEADS, HEAD_DIM),
            v.reshape(bsz, s, N_KV_HEADS, HEAD_DIM),
            attn_sink[l]).astype(x.dtype)

        gates = jax.nn.sigmoid(z + b_gate[l]).reshape(bsz, s, N_BRANCH, D_MODEL)
        merged = gates[:, :, 0] * y_a + gates[:, :, 1] * y_b
        x = x + merged @ w_out[l]

        xn2 = rmsnorm(x, norm_ffn_g[l])
        gu = xn2 @ w_ffn_in[l]
        ff_gate, ff_up = jnp.split(gu, [D_FF], axis=-1)
        x = x + (jax.nn.silu(ff_gate) * ff_up) @ w_ffn_out[l]
    return rmsnorm(x, norm_final_g)
```

```python
import numpy as np
import concourse.bass as bass
import concourse.mybir as mybir
from concourse.bass_utils import run_bass_kernel_spmd
from contextlib import ExitStack

F32 = mybir.dt.float32
BF16 = mybir.dt.bfloat16
I32 = mybir.dt.int32
AF = mybir.ActivationFunctionType
ALU = mybir.AluOpType

D = 1024
T = 2048
TH = 2304
NT = 18
KC = 8
FC = 22
NV = 120
NFLAG = 11
NS = 3
SR = 2176
SV = 72
SB_BASE = 16640
SB_END = 229376
EPS = 1e-6


class Prog:
    COMPUTE = ("pe", "act", "dve", "pool")

    def __init__(self, nc):
        self.nc = nc
        self.ops = []
        self.lastw = {}
        self.readers = {}
        self.pending = {}
        self.since_barrier = []

    def barrier(self):
        last = {}
        for i in self.since_barrier:
            op = self.ops[i]
            if op["dsem"] is not None:
                last[("d", id(op["dsem"]))] = i
            else:
                last[("e", op["eng"])] = i
        deps = set(last.values())
        for e in ("pe", "act", "dve", "pool", "sp"):
            self.pending.setdefault(e, set()).update(deps)
        self.since_barrier = []

    def add(self, eng, fn, r=(), w=(), dsem=None, ninc=16):
        idx = len(self.ops)
        deps = set()
        rawdeps = set()
        for k in r:
            if k in self.lastw:
                deps.add(self.lastw[k])
                rawdeps.add(self.lastw[k])
        for k in w:
            if k in self.lastw:
                deps.add(self.lastw[k])
            for x in self.readers.get(k, ()):
                deps.add(x)
        keep = []
        for d in deps:
            od = self.ops[d]
            same = (od["eng"] == eng and od["dsem"] is None and dsem is None)
            if same:
                if eng == "pe":
                    continue
                if d not in rawdeps:
                    continue
            keep.append(d)
        pend = self.pending.pop(eng, None)
        if pend:
            for d in pend:
                od = self.ops[d]
                if od["eng"] == eng and od["dsem"] is None:
                    continue
                if d not in keep:
                    keep.append(d)
        op = dict(eng=eng, fn=fn, deps=keep, dsem=dsem, ninc=ninc, inc=False, idx=idx)
        self.ops.append(op)
        self.since_barrier.append(idx)
        for d in keep:
            self.ops[d]["inc"] = True
        for k in w:
            self.lastw[k] = idx
            self.readers[k] = []
        for k in r:
            if k not in w:
                self.readers.setdefault(k, []).append(idx)
        return idx

    def emit(self, st, final_waits=()):
        nc = self.nc
        esem = self.esem
        cnt = {}
        for op in self.ops:
            if op["dsem"] is not None:
                s = op["dsem"]
                cnt[id(s)] = cnt.get(id(s), 0) + op["ninc"]
                op["csem"] = s
                op["cval"] = cnt[id(s)]
            elif op["inc"]:
                s = esem[op["eng"]]
                cnt[id(s)] = cnt.get(id(s), 0) + 1
                op["csem"] = s
                op["cval"] = cnt[id(s)]
            else:
                op["csem"] = None
        queues = {}
        for op in self.ops:
            queues.setdefault(op["eng"], []).append(op)
        ops = self.ops
        block = st.enter_context(nc.Block())

        def run(engname, eng):
            waited = {}
            for op in queues.get(engname, []):
                need = {}
                for d in op["deps"]:
                    od = ops[d]
                    s = od["csem"]
                    k = id(s)
                    if k not in need or need[k][1] < od["cval"]:
                        need[k] = (s, od["cval"])
                for k, (s, v) in need.items():
                    if waited.get(k, 0) >= v:
                        continue
                    eng.wait_ge(s, v)
                    waited[k] = v
                ins = op["fn"](eng)
                if op["dsem"] is None and op["inc"]:
                    ins.then_inc(op["csem"], 1)
            if engname == "sp":
                for d in final_waits:
                    od = ops[d]
                    eng.wait_ge(od["csem"], od["cval"])

        @block.sync
        def _(e):
            run("sp", e)

        @block.tensor
        def _(e):
            run("pe", e)

        @block.scalar
        def _(e):
            run("act", e)

        @block.vector
        def _(e):
            run("dve", e)

        @block.gpsimd
        def _(e):
            run("pool", e)


class Arena:
    def __init__(self, nc):
        self.nc = nc
        self.off = SB_BASE
        self.limit = SB_END
        self.n = 0

    def seek(self, off, limit=SB_END):
        self.off = off
        self.limit = limit

    def alloc(self, shape, dt):
        esz = 4 if dt in (F32, I32) else 2
        nb = int(np.prod(shape[1:])) * esz
        nb = (nb + 63) // 64 * 64
        assert self.off + nb <= self.limit, ("SBUF overflow", self.off, nb, self.limit)
        self.n += 1
        h = self.nc.alloc_sbuf_tensor_at("t%d" % self.n, list(shape), dt, offset=self.off)
        self.off += nb
        return h


def build_program(debug=False, stop=None):
    nc = bass.Bass("TRN2", target_bir_lowering=False)

    def din(name, shape, dt=F32):
        return nc.dram_tensor(name, list(shape), dt, kind="ExternalInput").ap()

    x_ext = din("x_ext", [TH, D])
    w_in = din("w_in", [D, 5632])
    w_out = din("w_out", [D, D])
    w_f1 = din("w_f1", [D, 5632])
    w_f2 = din("w_f2", [2816, D])
    wbd = din("wbd", [4, 128, 8, 128])
    vecs_d = din("vecs", [128, NV])
    gfin_d = din("gfin", [1, D])
    sink_d = din("sink", [1, 16])
    flags_d = din("flags", [128, NFLAG])
    xslot = din("xslot", [NS, SR, D])
    svecs_d = din("svecs", [128, NS * SV])
    swbd = din("swbd", [NS * 2, 128, 8, 128])
    out_d = nc.dram_tensor("out", [T, D], F32, kind="ExternalOutput").ap()

    P = Prog(nc)
    A = Arena(nc)
    st = ExitStack()
    nsem = [0]

    P.esem = {e: nc.alloc_semaphore(name="s_" + e) for e in Prog.COMPUTE}
    SEMS = [nc.alloc_semaphore(name="d%d" % i) for i in range(12)]
    for s_ in list(P.esem.values()) + SEMS:
        nc.gpsimd.sem_clear(s_)
    nc.all_engine_barrier()
    HWL = [1, 2, 3, 4, 9, 10]
    SWL = [5, 6, 7, 8]
    hwptr = [0]
    swptr = [0]

    def newsem(kind="hw"):
        if kind == "hw":
            s_ = SEMS[HWL[hwptr[0]]]
            hwptr[0] += 1
        else:
            s_ = SEMS[SWL[swptr[0]]]
            swptr[0] += 1
        return s_

    def phase_sems():
        hwptr[0] = 0
        swptr[0] = 0

    ps = [nc.alloc_psum_tensor("ps%d" % i, [128, 512], F32) for i in range(8)]

    def dma(q, out, in_, sem, r=(), w=()):
        return P.add(q, lambda e: e.dma_start(out=out, in_=in_).then_inc(sem, 16), r=r, w=w, dsem=sem)

    def dma_multi(q, pairs, sem, r=(), w=()):
        def fn(e):
            for (o, i) in pairs:
                ins = e.dma_start(out=o, in_=i).then_inc(sem, 16)
            return ins
        return P.add(q, fn, r=r, w=w, dsem=sem, ninc=16 * len(pairs))

    def K(b):
        return [(b, i) for i in range(4)]

    def finish_early(src_ap):
        sem_f = SEMS[11]
        P.barrier()
        f = dma("sp", out_d[0:128, 0:src_ap.shape[-1]], src_ap, sem_f)
        P.emit(st, final_waits=[f])
        st.close()
        nc.all_engine_barrier()
        for s_ in list(P.esem.values()) + SEMS:
            nc.gpsimd.sem_clear(s_)
        nc.all_engine_barrier()
        return nc

    ident = A.alloc([128, 128], BF16)
    ones = A.alloc([128, 128], BF16)
    onesLR = A.alloc([128, 2, 128], BF16)
    vecs = A.alloc([128, NV], F32)
    flags = A.alloc([128, NFLAG], F32)
    clv = A.alloc([128, 32], F32)
    es_row = A.alloc([1, 4, 4, 128], BF16)
    sinkt = A.alloc([1, 32], F32)
    svecs = A.alloc([128, NS * SV], F32)
    scl = A.alloc([128, 48], F32)
    fstate = A.alloc([128, NS + 1, 8], F32)
    sinit = A.alloc([128, NS, 8], F32)
    hinv = A.alloc([128, 2, 8], F32)
    ss = A.alloc([128, 128], F32)
    rt = A.alloc([128, 128], F32)
    rstd = A.alloc([128, 128], F32)
    iota_i = A.alloc([128, 384], I32)
    identf = A.alloc([128, 128], F32)
    sem_c = SEMS[0]
    C_END = A.off
    PA_OFF = C_END
    XNT_OFF = PA_OFF + 32768
    YBG_OFF = XNT_OFF + 36864
    X2_OFF = YBG_OFF + 32768
    X2_END = X2_OFF + 65536

    G1C, CW, CB, LAM, BA, BX, BG, G2C = 0, 8, 40, 48, 64, 80, 96, 112

    dma_multi("sp", [(vecs[:], vecs_d), (flags[:], flags_d), (sinkt[:, 0:16], sink_d), (svecs[:], svecs_d)], sem_c, w=["vecs", "flags", "sinkt", "svecs"])
    P.add("pool", lambda e: e.memset(fstate[:], 0.0), w=["finals0"])
    for s_i in range(NS):
        P.add("act", lambda e, s_i=s_i: e.activation(out=scl[:, s_i * 8:s_i * 8 + 8], in_=svecs[:, s_i * SV + 48:s_i * SV + 56], func=AF.Exp, scale=-1.0), r=["svecs"], w=[("scl0", s_i)])
        P.add("act", lambda e, s_i=s_i: e.activation(out=scl[:, 24 + s_i * 8:24 + s_i * 8 + 8], in_=scl[:, s_i * 8:s_i * 8 + 8], func=AF.Ln, bias=1.0), r=[("scl0", s_i)], w=[("scl1", s_i)])
        P.add("dve", lambda e, s_i=s_i: e.tensor_scalar(out=scl[:, s_i * 8:s_i * 8 + 8], in0=scl[:, 24 + s_i * 8:24 + s_i * 8 + 8], scalar1=-8.0, scalar2=None, op0=ALU.mult), r=[("scl1", s_i)], w=[("scl2", s_i)])
        P.add("dve", lambda e, s_i=s_i: e.tensor_scalar(out=scl[:, 24 + s_i * 8:24 + s_i * 8 + 8], in0=scl[:, 24 + s_i * 8:24 + s_i * 8 + 8], scalar1=-16.0, scalar2=None, op0=ALU.mult), r=[("scl1", s_i), ("scl2", s_i)], w=[("scl", s_i)])
    P.add("pool", lambda e: e.memset(ones[:], 1.0), w=["ones"])
    P.add("pool", lambda e: e.memset(ss[:], 0.0), w=["ss"])
    P.add("pool", lambda e: e.iota(iota_i[:, 0:128], pattern=[[1, 128]], base=0, channel_multiplier=-1), w=["iota"])
    P.add("dve", lambda e: e.tensor_copy(out=identf[:], in_=iota_i[:, 0:128]), r=["iota"], w=["identf"])
    P.add("dve", lambda e: e.tensor_scalar(out=ident[:], in0=identf[:], scalar1=0.0, scalar2=None, op0=ALU.is_equal),
          r=["identf"], w=["ident"])
    P.add("dve", lambda e: e.tensor_tensor(out=onesLR[:], in0=ones[:].unsqueeze(1).broadcast_to([128, 2, 128]),
                                           in1=flags[:, 0:2].unsqueeze(2).broadcast_to([128, 2, 128]), op=ALU.mult),
          r=["ones", "flags"], w=["onesLR"])
    P.add("act", lambda e: e.activation(out=clv[:, 0:16], in_=vecs[:, LAM:LAM + 16], func=AF.Exp, scale=-1.0), r=["vecs"], w=["cl0"])
    P.add("act", lambda e: e.activation(out=clv[:, 16:32], in_=clv[:, 0:16], func=AF.Ln, bias=1.0), r=["cl0"], w=["cl1"])
    P.add("dve", lambda e: e.tensor_scalar(out=clv[:, 0:16], in0=clv[:, 16:32], scalar1=-8.0, scalar2=None, op0=ALU.mult), r=["cl1"], w=["cl2"])
    P.add("dve", lambda e: e.tensor_scalar(out=clv[:, 16:32], in0=clv[:, 16:32], scalar1=-16.0, scalar2=None, op0=ALU.mult), r=["cl1", "cl2"], w=["cl"])
    P.add("act", lambda e: e.activation(out=sinkt[:, 16:32], in_=sinkt[:, 0:16], func=AF.Exp), r=["sinkt"], w=["esink"])
    P.add("dve", lambda e: e.tensor_copy(out=es_row[:], in_=sinkt[:, 16:32].rearrange("p (g h) -> p g h", g=4).unsqueeze(3).broadcast_to([1, 4, 4, 128])),
          r=["esink"], w=["es_row"])

    A.seek(PA_OFF)
    Pa = A.alloc([128, KC, T], BF16)
    xnT = A.alloc([128, KC, TH], BF16)
    assert A.off == YBG_OFF
    W_OFF = A.off

    psrot = [0]

    def nextbank(lo=0, hi=8):
        b = lo + psrot[0] % (hi - lo)
        psrot[0] += 1
        return b

    def norm_a(src_ap, src_keys, sscol, sq_t, sq_key):
        P.add("act", lambda e: e.activation(out=sq_t[:], in_=src_ap, func=AF.Square, accum_out=ss[:, sscol:sscol + 1]),
              r=list(src_keys) + ["ss"], w=[("ssc", sscol), sq_key])
        P.add("act", lambda e: e.activation(out=rt[:, sscol:sscol + 1], in_=ss[:, sscol:sscol + 1], func=AF.Sqrt, scale=1.0 / D, bias=EPS),
              r=[("ssc", sscol)], w=[("rt", sscol)])
        P.add("dve", lambda e: e.reciprocal(out=rstd[:, sscol:sscol + 1], in_=rt[:, sscol:sscol + 1]), r=[("rt", sscol)], w=[("rstd", sscol)])

    def norm_b(src_ap, src_keys, sscol, xs_t, xs_key, gcol, dstT, dcol0, dkey):
        P.add("act", lambda e: e.mul(out=xs_t[:], in_=src_ap, mul=rstd[:, sscol:sscol + 1]),
              r=list(src_keys) + [("rstd", sscol)], w=[xs_key])
        for hb in range(2):
            bk = nextbank()

            def fn(e, hb=hb, bk=bk):
                for k in range(4):
                    c = hb * 4 + k
                    ins = e.matmul(ps[bk][:, k * 128:(k + 1) * 128], lhsT=xs_t[:, c * 128:(c + 1) * 128], rhs=ident[:], start=True, stop=True)
                return ins
            P.add("pe", fn, r=[xs_key, "ident"], w=[("ps", bk)])
            P.add("dve", lambda e, hb=hb, bk=bk: e.tensor_tensor(
                out=dstT[:, hb * 4:(hb + 1) * 4, dcol0:dcol0 + 128],
                in0=ps[bk][:, :].rearrange("p (a b) -> p a b", a=4),
                in1=vecs[:, gcol + hb * 4:gcol + hb * 4 + 4].unsqueeze(2).broadcast_to([128, 4, 128]), op=ALU.mult),
                r=[("ps", bk), "vecs"], w=[(dkey, dcol0, hb)])

    def proj_fm(wt_ap_fn, wkey, rhsT, col0, n, evac):
        bk = nextbank()

        def fn(e):
            for kc in range(KC):
                ins = e.matmul(ps[bk][:, 0:n], lhsT=wt_ap_fn(kc), rhs=rhsT[:, kc, col0:col0 + n], start=(kc == 0), stop=(kc == KC - 1))
            return ins
        P.add("pe", fn, r=[wkey], w=[("ps", bk)])
        evac(bk)


    A.seek(XNT_OFF, YBG_OFF)
    xsT = A.alloc([128, KC, SR], BF16)
    A.seek(W_OFF)
    S_XT = [A.alloc([128, D], F32) for _ in range(6)]
    S_XS = [A.alloc([128, D], BF16) for _ in range(2)]
    S_SQJ = A.alloc([128, D], BF16)
    wlu = [A.alloc([128, KC, 128], BF16) for _ in range(2)]
    swbd_s = A.alloc([128, NS * 2, 8, 128], BF16)
    S_UT = [A.alloc([128, 2052], F32) for _ in range(2)]
    S_BF = [[A.alloc([128, T], F32) for _ in range(4)] for _ in range(2)]
    A.seek(PA_OFF, XNT_OFF)
    S_UC = [A.alloc([128, T], F32) for _ in range(2)]
    S_UCB = [A.alloc([128, T], BF16) for _ in range(2)]
    phase_sems()
    sem_x = [newsem() for _ in range(6)]
    sem_wlu = [newsem('sw') for _ in range(2)]
    sem_sbd = newsem('sw')
    w_in_v = w_in.rearrange("(kc p) n -> p kc n", p=128)
    dma_multi("pool", [(swbd_s[:, i, :, :], swbd[i]) for i in range(NS * 2)], sem_sbd, w=["swbd"])
    SUB = ((0, 512), (512, 512), (1024, 512), (1536, 512), (2048, 4))
    slot_rot = [0]

    def KP(name, par):
        return [((name, par), i) for i in range(4)]

    def slot_s1(s_i, c):
        sl = c % 2
        dma("pool", wlu[sl][:], w_in_v[:, :, c * 128:(c + 1) * 128], sem_wlu[sl], w=[("wlu", sl)])
        u_t = S_UT[sl]
        uc_, ucb_ = S_UC[sl], S_UCB[sl]
        for bi, (j0, n) in enumerate(SUB):
            def ev(bk, j0=j0, n=n, bi=bi):
                if bi % 2 == 0:
                    P.add("act", lambda e: e.copy(out=u_t[:, j0:j0 + n], in_=ps[bk][:, 0:n]), r=[("ps", bk)], w=[("suT", sl, j0)])
                else:
                    P.add("dve", lambda e: e.tensor_copy(out=u_t[:, j0:j0 + n], in_=ps[bk][:, 0:n]), r=[("ps", bk)], w=[("suT", sl, j0)])
            proj_fm(lambda kc: wlu[sl][:, kc, :], ("wlu", sl), xsT, j0, n, ev)
        ukeys = [("suT", sl, j0) for (j0, n) in SUB]
        vb = s_i * SV
        cwa = [svecs[:, vb + k * 8 + c:vb + k * 8 + c + 1] for k in range(5)]
        cba = svecs[:, vb + 40 + c:vb + 40 + c + 1]
        P.add("dve", lambda e: e.tensor_scalar(out=uc_[:], in0=u_t[:, 0:T], scalar1=cwa[0], scalar2=cba, op0=ALU.mult, op1=ALU.add),
              r=ukeys + ["svecs"], w=[("suc", sl)])
        for k in (1, 2, 3):
            P.add("dve", lambda e, k=k: e.scalar_tensor_tensor(out=uc_[:], in0=u_t[:, k:k + T], scalar=cwa[k], in1=uc_[:], op0=ALU.mult, op1=ALU.add),
                  r=ukeys + [("suc", sl)], w=[("suc", sl)])

    def slot_s1b(s_i, c):
        sl = c % 2
        uc_, ucb_ = S_UC[sl], S_UCB[sl]
        P.add("act", lambda e: e.copy(out=ucb_[:], in_=uc_[:]), r=[("suc", sl)], w=[("sucb", sl)])

    def slot_s2(s_i, c):
        sl = c % 2
        uc_, ucb_ = S_UC[sl], S_UCB[sl]
        rb, ib, ab, hb_ = S_BF[sl]
        vb = s_i * SV
        for gate, dst, boff, nm in ((0, rb, 56, "B0"), (1, ib, 64, "B1")):
            bias_ap = svecs[:, vb + boff + c:vb + boff + c + 1]
            for tb in range(4):
                bk = nextbank()
                P.add("pe", lambda e, bk=bk, gate=gate, tb=tb: e.matmul(ps[bk][:, :], lhsT=swbd_s[:, s_i * 2 + gate, c, :], rhs=ucb_[:, tb * 512:(tb + 1) * 512], start=True, stop=True),
                      r=["swbd", ("sucb", sl)], w=[("ps", bk)])
                P.add("act", lambda e, bk=bk, dst=dst, tb=tb, bias_ap=bias_ap: e.activation(out=dst[:, tb * 512:(tb + 1) * 512], in_=ps[bk][:, :], func=AF.Sigmoid, bias=bias_ap),
                      r=[("ps", bk), "svecs"], w=[((nm, sl), tb)])
        ci = s_i * 8 + c
        P.add("act", lambda e: e.activation(out=ab[:], in_=rb[:], func=AF.Exp, scale=scl[:, ci:ci + 1]), r=KP("B0", sl) + [("scl", s_i)], w=KP("B2", sl))
        P.add("act", lambda e: e.activation(out=rb[:], in_=rb[:], func=AF.Exp, scale=scl[:, 24 + ci:24 + ci + 1]), r=KP("B0", sl) + [("scl", s_i)], w=KP("B0", sl))
        P.add("act", lambda e: e.activation(out=rb[:], in_=rb[:], func=AF.Sqrt, scale=-1.0, bias=1.0), r=KP("B0", sl), w=KP("B0", sl))
        P.add("dve", lambda e: e.tensor_tensor(out=ib[:], in0=rb[:], in1=ib[:], op=ALU.mult), r=KP("B0", sl) + KP("B1", sl), w=KP("B1", sl))
        P.add("dve", lambda e: e.tensor_tensor(out=ib[:], in0=ib[:], in1=uc_[:], op=ALU.mult), r=KP("B1", sl) + [("suc", sl)], w=KP("B1", sl))
        P.add("dve", lambda e: e.tensor_tensor_scan(out=hb_[:, :], data0=ab[:, :], data1=ib[:, :], initial=sinit[:, s_i, c:c + 1], op0=ALU.mult, op1=ALU.add),
              r=KP("B2", sl) + KP("B1", sl) + [("sinit", s_i)], w=KP("B3", sl))
        P.add("dve", lambda e: e.tensor_copy(out=fstate[:, s_i + 1, c:c + 1], in_=hb_[:, T - 1:T]), r=KP("B3", sl) + ["finals0"], w=[("fin", s_i, c)])

    for s_i in range(NS):
        nti = SR // 128
        for i in range(nti + 1):
            if i < nti:
                s3 = i % 6
                dma("sp", S_XT[s3][:], xslot[s_i, i * 128:(i + 1) * 128, :], sem_x[s3], w=[("xt", s3)])
                norm_a(S_XT[s3][:], [("xt", s3)], 64 + (s_i * 17 + i) % 64, S_SQJ, "sqj")
            if i >= 1:
                j = i - 1
                norm_b(S_XT[j % 6][:], [("xt", j % 6)], 64 + (s_i * 17 + j) % 64, S_XS[j % 2], ("xs", j % 2), G1C, xsT, j * 128, "xsT")
        P.add("dve", lambda e, s_i=s_i: e.tensor_scalar(out=sinit[:, s_i, :], in0=fstate[:, s_i, :], scalar1=flags[:, 2 + s_i:3 + s_i], scalar2=None, op0=ALU.mult),
              r=["flags", "finals0"] + [("fin", s_i - 1, c) for c in range(KC) if s_i > 0], w=[("sinit", s_i)])
        P.barrier()
        slot_s1(s_i, 0)
        slot_s1b(s_i, 0)
        for c in range(KC):
            if c + 1 < KC:
                slot_s1(s_i, c + 1)
            slot_s2(s_i, c)
            if c + 1 < KC:
                slot_s1b(s_i, c + 1)
        P.barrier()
    for d_ in range(2):
        fc = 5 + 3 * d_
        P.add("dve", lambda e, d_=d_, fc=fc: e.tensor_scalar(out=hinv[:, d_, :], in0=fstate[:, 1, :], scalar1=flags[:, fc:fc + 1], scalar2=None, op0=ALU.mult),
              r=["flags"], w=[("hinv", d_)])
        for s_i in (1, 2):
            P.add("dve", lambda e, d_=d_, fc=fc, s_i=s_i: e.scalar_tensor_tensor(out=hinv[:, d_, :], in0=fstate[:, s_i + 1, :], scalar=flags[:, fc + s_i:fc + s_i + 1], in1=hinv[:, d_, :], op0=ALU.mult, op1=ALU.add),
                  r=["flags", ("hinv", d_)], w=[("hinv", d_)])
    P.barrier()
    if stop == 'S':
        return finish_early(hinv[:].rearrange('p a b -> p (a b)'))

    A.seek(W_OFF)
    xt = [A.alloc([128, D], F32) for _ in range(6)]
    xs = [A.alloc([128, D], BF16) for _ in range(2)]
    sqj = A.alloc([128, D], BF16)
    phase_sems()
    sem_x = [newsem() for _ in range(6)]
    for i in range(NT):
        s3 = i % 6
        dma("sp", xt[s3][:], x_ext[i * 128:(i + 1) * 128, :], sem_x[s3], w=[("xt", s3)])
        norm_a(xt[s3][:], [("xt", s3)], i, sqj, "sqj")
        if i >= 1:
            norm_b(xt[(i - 1) % 6][:], [("xt", (i - 1) % 6)], i - 1, xs[(i - 1) % 2], ("xs", (i - 1) % 2), G1C, xnT, (i - 1) * 128, "xnT")
    norm_b(xt[(NT - 1) % 6][:], [("xt", (NT - 1) % 6)], NT - 1, xs[(NT - 1) % 2], ("xs", (NT - 1) % 2), G1C, xnT, (NT - 1) * 128, "xnT")
    P.barrier()
    if stop == 'A':
        return finish_early(xnT[:, 0:4, 0:128].bitcast(F32) if False else xt[0][:])

    A.seek(W_OFF)
    wl = [A.alloc([128, KC, 3, 128], BF16) for _ in range(2)]
    wbd_s = A.alloc([128, 4, 8, 128], BF16)
    uT = [A.alloc([128, 2052], F32) for _ in range(2)]
    uc2 = [A.alloc([128, T], F32) for _ in range(2)]
    ucb2 = [A.alloc([128, T], BF16) for _ in range(2)]
    G12 = [A.alloc([128, T], BF16) for _ in range(2)]
    gel = A.alloc([128, T], BF16)
    gA = A.alloc([128, T], BF16)
    Bf = [A.alloc([128, T], F32) for _ in range(6)]
    phase_sems()
    sem_wl = [newsem('sw') for _ in range(2)]
    sem_bd = newsem('sw')
    w_in_v = w_in.rearrange("(kc p) n -> p kc n", p=128)

    dma_multi("pool", [(wbd_s[:, i, :, :], wbd[i]) for i in range(4)], sem_bd, w=["wbd"])

    UB = ((0, 512), (512, 512), (1024, 512), (1536, 512), (2048, 3))

    def lru_bufs(d):
        return (Bf[3 * d], Bf[3 * d + 1], Bf[3 * d + 2]), ("B%d" % (3 * d), "B%d" % (3 * d + 1), "B%d" % (3 * d + 2))

    def lru_gates(c, d):
        par = c % 2
        ucb = ucb2[par]
        (rb, ib, ab), (kr, ki, ka) = lru_bufs(d)
        for gate, dst, bcol, nm in ((0, rb, BA, kr), (1, ib, BX, ki)):
            bias_ap = vecs[:, bcol + d * 8 + c:bcol + d * 8 + c + 1]
            for tb in range(4):
                bk = nextbank()
                P.add("pe", lambda e, bk=bk, gate=gate, tb=tb: e.matmul(ps[bk][:, :], lhsT=wbd_s[:, d * 2 + gate, c, :], rhs=ucb[:, tb * 512:(tb + 1) * 512], start=True, stop=True),
                      r=["wbd", ("ucb", par)], w=[("ps", bk)])
                P.add("act", lambda e, bk=bk, dst=dst, tb=tb, bias_ap=bias_ap: e.activation(out=dst[:, tb * 512:(tb + 1) * 512], in_=ps[bk][:, :], func=AF.Sigmoid, bias=bias_ap),
                      r=[("ps", bk), "vecs"], w=[(nm, tb)])

    def lru_exp(c, d):
        (rb, ib, ab), (kr, ki, ka) = lru_bufs(d)
        ci = d * 8 + c
        P.add("act", lambda e: e.activation(out=ab[:], in_=rb[:], func=AF.Exp, scale=clv[:, ci:ci + 1]), r=K(kr) + ["cl"], w=K(ka))
        P.add("act", lambda e: e.activation(out=rb[:], in_=rb[:], func=AF.Exp, scale=clv[:, 16 + ci:16 + ci + 1]), r=K(kr) + ["cl"], w=K(kr))

    def lru_sqrt(c, d):
        (rb, ib, ab), (kr, ki, ka) = lru_bufs(d)
        P.add("act", lambda e: e.activation(out=rb[:], in_=rb[:], func=AF.Sqrt, scale=-1.0, bias=1.0), r=K(kr), w=K(kr))

    def lru_scan(c, d):
        par = c % 2
        uc = uc2[par]
        (rb, ib, ab), (kr, ki, ka) = lru_bufs(d)
        P.add("dve", lambda e: e.tensor_tensor(out=ib[:], in0=rb[:], in1=ib[:], op=ALU.mult), r=K(kr) + K(ki), w=K(ki))
        P.add("dve", lambda e: e.tensor_tensor(out=ib[:], in0=ib[:], in1=uc[:], op=ALU.mult), r=K(ki) + [("uc", par)], w=K(ki))
        rev = (lambda ap: ap[:, ::-1]) if d == 1 else (lambda ap: ap[:, :])
        e0 = 0 if d == 0 else T - 1
        P.add("dve", lambda e: e.scalar_tensor_tensor(out=ib[:, e0:e0 + 1], in0=ab[:, e0:e0 + 1], scalar=hinv[:, d, c:c + 1], in1=ib[:, e0:e0 + 1], op0=ALU.mult, op1=ALU.add),
              r=K(ka) + K(ki) + [("hinv", d)], w=K(ki))
        P.add("dve", lambda e: e.tensor_tensor_scan(out=rev(rb), data0=rev(ab), data1=rev(ib), initial=0.0, op0=ALU.mult, op1=ALU.add),
              r=K(ka) + K(ki) + K(kr), w=K(kr))

    def lru_s1(c):
        sl = c % 2
        uc = uc2[sl]
        cols = [c * 128, 1024 + c * 128, 3584 + c * 128]
        dma_multi("pool", [(wl[sl][:, :, j, :], w_in_v[:, :, cols[j]:cols[j] + 128]) for j in range(3)], sem_wl[sl], w=[("wl", sl)])
        u_t = uT[sl]
        for bi, (j0, n) in enumerate(UB):
            def ev(bk, j0=j0, n=n, bi=bi):
                if bi % 2 == 0:
                    P.add("act", lambda e: e.copy(out=u_t[:, j0:j0 + n], in_=ps[bk][:, 0:n]), r=[("ps", bk)], w=[("uT", sl, j0)])
                else:
                    P.add("dve", lambda e: e.tensor_copy(out=u_t[:, j0:j0 + n], in_=ps[bk][:, 0:n]), r=[("ps", bk)], w=[("uT", sl, j0)])
            proj_fm(lambda kc: wl[sl][:, kc, 0, :], ("wl", sl), xnT, 126 + j0, n, ev)
        ukeys = [("uT", sl, j0) for (j0, n) in UB]
        cwa = [vecs[:, CW + k * 8 + c:CW + k * 8 + c + 1] for k in range(4)]
        cba = vecs[:, CB + c:CB + c + 1]
        P.add("dve", lambda e: e.tensor_scalar(out=uc[:], in0=u_t[:, 0:T], scalar1=cwa[0], scalar2=cba, op0=ALU.mult, op1=ALU.add),
              r=ukeys + ["vecs"], w=[("uc", sl)])
        for k in (1, 2, 3):
            P.add("dve", lambda e, k=k: e.scalar_tensor_tensor(out=uc[:], in0=u_t[:, k:k + T], scalar=cwa[k], in1=uc[:], op0=ALU.mult, op1=ALU.add),
                  r=ukeys + [("uc", sl)], w=[("uc", sl)])
        for tb in range(4):
            def evg(bk, tb=tb):
                P.add("act", lambda e: e.activation(out=gel[:, tb * 512:(tb + 1) * 512], in_=ps[bk][:, :], func=AF.Gelu_apprx_tanh), r=[("ps", bk)], w=[("gel", tb)])
            proj_fm(lambda kc: wl[sl][:, kc, 1, :], ("wl", sl), xnT, 128 + tb * 512, 512, evg)
        bga = vecs[:, BG + c:BG + c + 1]
        for tb in range(4):
            def evz(bk, tb=tb):
                P.add("act", lambda e: e.activation(out=gA[:, tb * 512:(tb + 1) * 512], in_=ps[bk][:, :], func=AF.Sigmoid, bias=bga),
                      r=[("ps", bk), "vecs"], w=[("gA", tb)])
            proj_fm(lambda kc: wl[sl][:, kc, 2, :], ("wl", sl), xnT, 128 + tb * 512, 512, evz)
        P.add("dve", lambda e: e.tensor_tensor(out=G12[sl][:], in0=gel[:], in1=gA[:], op=ALU.mult), r=K("gel") + K("gA"), w=[("G1", sl)])

    def lru_s1b(c):
        sl = c % 2
        P.add("act", lambda e: e.copy(out=ucb2[sl][:], in_=uc2[sl][:]), r=[("uc", sl)], w=[("ucb", sl)])

    def lru_s2(c):
        sl = c % 2
        lru_gates(c, 0)
        lru_gates(c, 1)
        lru_exp(c, 0)
        lru_exp(c, 1)
        lru_sqrt(c, 0)
        lru_sqrt(c, 1)
        lru_scan(c, 0)
        lru_scan(c, 1)
        P.add("dve", lambda e: e.tensor_tensor(out=Bf[0][:], in0=Bf[0][:], in1=Bf[3][:], op=ALU.add), r=K("B0") + K("B3"), w=K("B0"))
        P.add("dve", lambda e: e.tensor_tensor(out=Pa[:, c, :], in0=Bf[0][:], in1=G12[sl][:], op=ALU.mult), r=K("B0") + [("G1", sl)], w=[("Pa", c)])

    lru_s1(0)
    lru_s1b(0)
    for c in range(KC):
        if c + 1 < KC:
            lru_s1(c + 1)
        lru_s2(c)
        if c + 1 < KC:
            lru_s1b(c + 1)
    if stop in ('B', 'B1'):
        return finish_early(Bf[3][:, 0:1024])
    P.barrier()
    if stop == 'X':
        return finish_early(Bf[3][:, 0:1024])

    A.seek(YBG_OFF)
    ybg = A.alloc([128, KC, T], BF16)
    assert A.off == X2_OFF
    Et = A.alloc([128, 16, 384], BF16)
    absd = A.alloc([128, 384], F32)
    maskd = A.alloc([128, 384], F32)
    etmp = A.alloc([128, 384], F32)
    wq = A.alloc([128, KC, 256], BF16)
    wkk = A.alloc([128, KC, 128], BF16)
    wvv = A.alloc([128, KC, 128], BF16)
    wzb = A.alloc([128, KC, 256], BF16)
    qT = A.alloc([128, 2, T], BF16)
    kT = A.alloc([128, TH], BF16)
    vv = A.alloc([128, NT, 128], BF16)
    gB = A.alloc([128, 2, T], BF16)
    pT = [A.alloc([128, 4, 384], BF16) for _ in range(4)]
    ext = [A.alloc([128, 384], F32) for _ in range(3)]
    Rt = [A.alloc([128, 512], F32) for _ in range(2)]
    Tn = [A.alloc([128, 512], F32) for _ in range(2)]
    phase_sems()
    sem_wc = newsem('sw')

    P.add("pool", lambda e: e.iota(iota_i[:, :], pattern=[[1, 384]], base=-128, channel_multiplier=-1), w=["iota"])
    P.add("dve", lambda e: e.tensor_copy(out=absd[:], in_=iota_i[:]), r=["iota"], w=["absd"])
    P.add("dve", lambda e: e.tensor_scalar(out=maskd[:], in0=absd[:], scalar1=-1.0, scalar2=None, op0=ALU.mult), r=["absd"], w=["maskd"])
    P.add("dve", lambda e: e.tensor_tensor(out=absd[:], in0=absd[:], in1=maskd[:], op=ALU.max), r=["absd", "maskd"], w=["absd"])
    P.add("dve", lambda e: e.tensor_scalar(out=maskd[:], in0=absd[:], scalar1=128.0, scalar2=None, op0=ALU.is_le), r=["absd"], w=["maskd"])
    for h in range(16):
        slope = float(2.0 ** (-8.0 * (h + 1) / 16.0))
        P.add("act", lambda e, slope=slope: e.activation(out=etmp[:], in_=absd[:], func=AF.Exp, scale=-slope), r=["absd"], w=["etmp"])
        P.add("dve", lambda e, h=h: e.tensor_tensor(out=Et[:, h, :], in0=etmp[:], in1=maskd[:], op=ALU.mult), r=["etmp", "maskd"], w=[("Et", h)])

    exrot = [0]
    KB5 = ((0, 512), (512, 512), (1024, 512), (1536, 512), (2048, 256))

    def attn_group(g):
        pairs = [(wq[:], w_in_v[:, :, 2048 + g * 256:2048 + (g + 1) * 256]),
                 (wkk[:, :, 0:64], w_in_v[:, :, 3072 + g * 64:3072 + (g + 1) * 64]),
                 (wkk[:, :, 64:128], w_in_v[:, :, 3072 + g * 64:3072 + (g + 1) * 64]),
                 (wvv[:, :, 0:64], w_in_v[:, :, 3328 + g * 64:3328 + (g + 1) * 64]),
                 (wvv[:, :, 64:128], w_in_v[:, :, 3328 + g * 64:3328 + (g + 1) * 64]),
                 (wzb[:], w_in_v[:, :, 4608 + g * 256:4608 + (g + 1) * 256])]
        dma_multi("pool", pairs, sem_wc, w=["wC"])
        for j0, n in KB5:
            def evk(bk, j0=j0, n=n):
                P.add("dve", lambda e: e.tensor_copy(out=kT[:, j0:j0 + n], in_=ps[bk][:, 0:n]), r=[("ps", bk)], w=[("kT", j0)])
            proj_fm(lambda kc: wkk[:, kc, :], "wC", xnT, j0, n, evk)
        for i0 in range(0, NT, 4):
            nt = min(4, NT - i0)
            bk = nextbank()

            def fnv(e, i0=i0, nt=nt, bk=bk):
                for ii in range(nt):
                    for kc in range(KC):
                        ins = e.matmul(ps[bk][:, ii * 128:(ii + 1) * 128], lhsT=xnT[:, kc, (i0 + ii) * 128:(i0 + ii + 1) * 128], rhs=wvv[:, kc, :],
                                       start=(kc == 0), stop=(kc == KC - 1))
                return ins
            P.add("pe", fnv, r=["wC"], w=[("ps", bk)])
            P.add("act", lambda e, i0=i0, nt=nt, bk=bk: e.copy(out=vv[:, i0:i0 + nt, :], in_=ps[bk][:, 0:nt * 128].rearrange("p (a b) -> p a b", a=nt)),
                  r=[("ps", bk)], w=[("vv", i0)])
        for cq in range(2):
            for tb in range(4):
                def evq(bk, cq=cq, tb=tb):
                    P.add("act", lambda e: e.mul(out=qT[:, cq, tb * 512:(tb + 1) * 512], in_=ps[bk][:, :], mul=0.125), r=[("ps", bk)], w=[("qT", cq, tb)])
                proj_fm(lambda kc, cq=cq: wq[:, kc, cq * 128:(cq + 1) * 128], "wC", xnT, 128 + tb * 512, 512, evq)
        for cq in range(2):
            bc = BG + 8 + 2 * g + cq
            bza = vecs[:, bc:bc + 1]
            for tb in range(4):
                def evzb(bk, cq=cq, tb=tb, bza=bza):
                    P.add("act", lambda e: e.activation(out=gB[:, cq, tb * 512:(tb + 1) * 512], in_=ps[bk][:, :], func=AF.Sigmoid, bias=bza),
                          r=[("ps", bk), "vecs"], w=[("gB", cq, tb)])
                proj_fm(lambda kc, cq=cq: wzb[:, kc, cq * 128:(cq + 1) * 128], "wC", xnT, 128 + tb * 512, 512, evzb)
        qkeys = [("qT", cq, tb) for cq in range(2) for tb in range(4)]
        kkeys = [("kT", j0) for (j0, n) in KB5]
        vkeys = [("vv", i0) for i0 in range(0, NT, 4)]
        gkeys = [("gB", cq, tb) for cq in range(2) for tb in range(4)]

        def scores(kb):
            qb_lo = max(kb - 1, 1)
            qb_hi = min(kb + 1, 16)
            qlo = (qb_lo - 1) * 128
            n = (qb_hi - qb_lo + 1) * 128
            c0 = (qb_lo - (kb - 1)) * 128
            psl = kb % 4
            for cq in range(2):
                for half in range(2):
                    h = 4 * g + 2 * cq + half
                    slot = cq + 2 * half
                    bk = nextbank(0, 4)
                    r0 = half * 64
                    P.add("pe", lambda e, bk=bk, r0=r0, cq=cq: e.matmul(ps[bk][:, 0:n], lhsT=kT[r0:r0 + 64, kb * 128:(kb + 1) * 128], rhs=qT[r0:r0 + 64, cq, qlo:qlo + n],
                                                                        start=True, stop=True),
                          r=qkeys + kkeys, w=[("ps", bk)])
                    xi = exrot[0] % 3
                    exrot[0] += 1
                    P.add("act", lambda e, bk=bk, xi=xi: e.activation(out=ext[xi][:, 0:n], in_=ps[bk][:, 0:n], func=AF.Exp), r=[("ps", bk)], w=[("ext", xi)])
                    P.add("pool" if half == 1 else "dve", lambda e, xi=xi, slot=slot, h=h: e.tensor_tensor(out=pT[psl][:, slot, c0:c0 + n], in0=ext[xi][:, 0:n], in1=Et[:, h, c0:c0 + n], op=ALU.mult),
                          r=[("ext", xi), ("Et", h)], w=[("pT", psl, slot)])

        def pv(qb):
            bx = 4 + 2 * (qb % 2)
            by = bx + 1

            def fnx(e):
                for j, kb in enumerate((qb - 1, qb, qb + 1)):
                    dj = qb - kb + 1
                    ins = e.matmul(ps[bx][:, :].rearrange("p (a b) -> p a b", a=4), lhsT=vv[:, kb, :], rhs=pT[kb % 4][:, :, dj * 128:(dj + 1) * 128],
                                   start=(j == 0), stop=(j == 2))
                return ins

            def fny(e):
                for j, kb in enumerate((qb - 1, qb, qb + 1)):
                    dj = qb - kb + 1
                    lh = onesLR[:, 0, :] if kb == 0 else (onesLR[:, 1, :] if kb == NT - 1 else ones[:])
                    e.matmul(ps[by][:, :].rearrange("p (a b) -> p a b", a=4), lhsT=lh, rhs=pT[kb % 4][:, :, dj * 128:(dj + 1) * 128], start=(j == 0), stop=False)
                return e.matmul(ps[by][:, :], lhsT=ones[0:1, :], rhs=es_row[0:1, g, :, :].rearrange("p a b -> p (a b)"), start=False, stop=True)
            pkeys = [("pT", kb % 4, s) for kb in (qb - 1, qb, qb + 1) for s in range(4)]
            P.add("pe", fnx, r=pkeys + vkeys, w=[("ps", bx)])
            P.add("pe", fny, r=pkeys + ["ones", "onesLR", "es_row"], w=[("ps", by)])
            ri = qb % 2
            P.add("act", lambda e: e.activation(out=Rt[ri][:], in_=ps[by][:, :], func=AF.Ln), r=[("ps", by)], w=[("Rt", ri)])
            P.add("act", lambda e: e.activation(out=Rt[ri][:], in_=Rt[ri][:], func=AF.Exp, scale=-1.0), r=[("Rt", ri)], w=[("Rt", ri)])
            P.add("dve", lambda e: e.tensor_tensor(out=Tn[ri][:], in0=ps[bx][:, :], in1=Rt[ri][:], op=ALU.mult), r=[("ps", bx), ("Rt", ri)], w=[("Tn", ri)])
            q0 = (qb - 1) * 128
            for half in range(2):
                r0 = half * 64
                P.add("dve", lambda e, r0=r0, half=half: e.tensor_tensor(
                    out=ybg[r0:r0 + 64, 2 * g:2 * g + 2, q0:q0 + 128],
                    in0=Tn[ri][r0:r0 + 64, half * 256:(half + 1) * 256].rearrange("p (a b) -> p a b", a=2),
                    in1=gB[r0:r0 + 64, :, q0:q0 + 128], op=ALU.mult),
                    r=[("Tn", ri)] + gkeys, w=[("ybg", g, qb, half)])

        scores(0)
        scores(1)
        for kb in range(2, NT):
            scores(kb)
            if kb >= 3:
                pv(kb - 2)
        pv(NT - 3)
        pv(NT - 2)

    if stop == 'E':
        return finish_early(Rt[0][:].bitcast(F32) if False else absd[:, 0:384])
    for g in range(4):
        attn_group(g)
        if stop == 'C1':
            break
    P.barrier()
    if stop in ('C', 'C1'):
        return finish_early(Tn[0][:, :])

    A.seek(X2_OFF)
    x2 = A.alloc([128, 16, D], F32)
    assert A.off == X2_END
    t1 = [A.alloc([128, 512], F32) for _ in range(2)]
    xr = [A.alloc([128, D], F32) for _ in range(2)]
    A.seek(XNT_OFF, YBG_OFF)
    wo = A.alloc([128, KC, D], BF16)
    mT = [A.alloc([128, KC, 512], BF16) for _ in range(2)]
    phase_sems()
    sem_wo = newsem('sw')
    sem_xr = [newsem() for _ in range(2)]
    w_out_v = w_out.rearrange("(kc p) n -> p kc n", p=128)
    dma_multi("pool", [(wo[:, 0:4, :], w_out_v[:, 0:4, :]), (wo[:, 4:8, :], w_out_v[:, 4:8, :])], sem_wo, w=["wo"])
    t1rot = [0]

    def d1_group(tg):
        tc0 = tg * 512
        ms = tg % 2
        for c in range(KC):
            P.add("dve", lambda e, c=c: e.tensor_tensor(out=mT[ms][:, c, :], in0=Pa[:, c, tc0:tc0 + 512], in1=ybg[:, c, tc0:tc0 + 512], op=ALU.add),
                  r=[("Pa", c)], w=[("mT", ms, c)])
        if stop in ('D1a', 'D1q'):
            return
        for tl in range(4):
            tile = tg * 4 + tl
            xs_ = tile % 2
            dma("sp", xr[xs_][:], x_ext[128 + tile * 128:128 + (tile + 1) * 128, :], sem_xr[xs_], w=[("xr", xs_)])
            for nh in range(2):
                bk = nextbank()

                def fno(e, bk=bk, tl=tl, nh=nh):
                    for c in range(KC):
                        ins = e.matmul(ps[bk][:, :], lhsT=mT[ms][:, c, tl * 128:(tl + 1) * 128], rhs=wo[:, c, nh * 512:(nh + 1) * 512], start=(c == 0), stop=(c == KC - 1))
                    return ins
                P.add("pe", fno, r=[("mT", ms, c) for c in range(KC)] + ["wo"], w=[("ps", bk)])
                P.add("dve", lambda e, bk=bk, tile=tile, nh=nh, xs_=xs_: e.tensor_tensor(out=x2[:, tile, nh * 512:(nh + 1) * 512], in0=ps[bk][:, :], in1=xr[xs_][:, nh * 512:(nh + 1) * 512], op=ALU.add),
                      r=[("ps", bk), ("xr", xs_)], w=[("x2", tile, nh)])

    for tg in range(4):
        d1_group(tg)
        if stop == 'D1a':
            return finish_early(t1[0][:])
    P.barrier()
    if stop == 'D1':
        return finish_early(x2[:, 0, :])

    A.seek(C_END, X2_OFF)
    w2 = A.alloc([128, FC, D], BF16)
    hT = A.alloc([128, FC, 1024], BF16)
    wf = [A.alloc([128, KC, 2, 128], BF16) for _ in range(2)]
    xs2 = A.alloc([128, D], BF16)
    sqj2 = A.alloc([128, D], BF16)
    A.seek(X2_END)
    xn2T = A.alloc([128, KC, 1024], BF16)
    sg = [A.alloc([128, 512], F32) for _ in range(2)]
    ot = A.alloc([128, D], F32)
    gfin = A.alloc([128, D], F32)
    phase_sems()
    sem_w2 = newsem('sw')
    sem_wf = [newsem('sw') for _ in range(2)]
    sem_o = newsem()
    sem_g = newsem()
    w_f2_v = w_f2.rearrange("(j p) n -> p j n", p=128)
    dma_multi("pool", [(w2[:, 0:11, :], w_f2_v[:, 0:11, :]), (w2[:, 11:22, :], w_f2_v[:, 11:22, :])], sem_w2, w=["w2"])
    dma("sp", gfin[:], gfin_d.partition_broadcast(128), sem_g, w=["gfin"])
    w_f1_v = w_f1.rearrange("(kc p) n -> p kc n", p=128)
    finals = []
    wfrot = [0]
    sgrot = [0]

    def ffn_in(j, ws):
        dma_multi("pool", [(wf[ws][:, :, 0, :], w_f1_v[:, :, j * 128:(j + 1) * 128]),
                           (wf[ws][:, :, 1, :], w_f1_v[:, :, 2816 + j * 128:2816 + (j + 1) * 128])], sem_wf[ws], w=[("wf", ws)])
        xkeys = [("xn2T", tl * 128, hb) for tl in range(8) for hb in range(2)]
        for tgl in range(2):
            bg = nextbank()
            bu = nextbank()

            def fng(e, bk=bg, which=0, tgl=tgl):
                for kc in range(KC):
                    ins = e.matmul(ps[bk][:, :], lhsT=wf[ws][:, kc, which, :], rhs=xn2T[:, kc, tgl * 512:(tgl + 1) * 512], start=(kc == 0), stop=(kc == KC - 1))
                return ins

            def fnu(e, bk=bu, which=1, tgl=tgl):
                for kc in range(KC):
                    ins = e.matmul(ps[bk][:, :], lhsT=wf[ws][:, kc, which, :], rhs=xn2T[:, kc, tgl * 512:(tgl + 1) * 512], start=(kc == 0), stop=(kc == KC - 1))
                return ins
            P.add("pe", fng, r=[("wf", ws)] + xkeys, w=[("ps", bg)])
            P.add("pe", fnu, r=[("wf", ws)] + xkeys, w=[("ps", bu)])
            si = sgrot[0] % 2
            sgrot[0] += 1
            P.add("act", lambda e, bg=bg, si=si: e.activation(out=sg[si][:], in_=ps[bg][:, :], func=AF.Silu), r=[("ps", bg)], w=[("sg", si)])
            P.add("dve", lambda e, bu=bu, si=si, tgl=tgl: e.tensor_tensor(out=hT[:, j, tgl * 512:(tgl + 1) * 512], in0=sg[si][:], in1=ps[bu][:, :], op=ALU.mult),
                  r=[("sg", si), ("ps", bu)], w=[("hT", j, tgl)])

    def ffn_out(tile, tl):
        sscol = 40 + tile
        hkeys = [("hT", j, tl // 4) for j in range(FC)]
        for nh in range(2):
            bk = nextbank()

            def fnf(e, bk=bk, nh=nh):
                for j in range(FC):
                    ins = e.matmul(ps[bk][:, :], lhsT=hT[:, j, tl * 128:(tl + 1) * 128], rhs=w2[:, j, nh * 512:(nh + 1) * 512], start=(j == 0), stop=(j == FC - 1))
                return ins
            P.add("pe", fnf, r=hkeys + ["w2"], w=[("ps", bk)])
            P.add("dve", lambda e, bk=bk, nh=nh: e.tensor_tensor(out=x2[:, tile, nh * 512:(nh + 1) * 512], in0=ps[bk][:, :], in1=x2[:, tile, nh * 512:(nh + 1) * 512], op=ALU.add),
                  r=[("ps", bk), ("x2", tile, nh)], w=[("x2", tile, nh)])
        xk = [("x2", tile, 0), ("x2", tile, 1)]
        P.add("act", lambda e: e.activation(out=sqj2[:], in_=x2[:, tile, :], func=AF.Square, accum_out=ss[:, sscol:sscol + 1]),
              r=xk + ["ss"], w=[("ssc", sscol), "sqj2"])
        P.add("act", lambda e: e.activation(out=rt[:, sscol:sscol + 1], in_=ss[:, sscol:sscol + 1], func=AF.Sqrt, scale=1.0 / D, bias=EPS), r=[("ssc", sscol)], w=[("rt", sscol)])
        P.add("dve", lambda e: e.reciprocal(out=rstd[:, sscol:sscol + 1], in_=rt[:, sscol:sscol + 1]), r=[("rt", sscol)], w=[("rstd", sscol)])
        P.add("dve", lambda e: e.scalar_tensor_tensor(out=ot[:], in0=x2[:, tile, :], scalar=rstd[:, sscol:sscol + 1], in1=gfin[:], op0=ALU.mult, op1=ALU.mult),
              r=xk + [("rstd", sscol), "gfin"], w=["ot"])
        finals.append(dma("sp", out_d[tile * 128:(tile + 1) * 128, :], ot[:], sem_o, r=["ot"], w=[("out", tile)]))

    for half in range(2):
        for tl in range(8):
            tile = half * 8 + tl
            norm_a(x2[:, tile, :], [("x2", tile, 0), ("x2", tile, 1)], 20 + tile, sqj2, "sqj2")
            norm_b(x2[:, tile, :], [("x2", tile, 0), ("x2", tile, 1)], 20 + tile, xs2, "xs2", G2C, xn2T, tl * 128, "xn2T")
        for j in range(FC):
            ws = wfrot[0] % 2
            wfrot[0] += 1
            ffn_in(j, ws)
        for tl in range(8):
            ffn_out(half * 8 + tl, tl)

    P.emit(st, final_waits=finals)
    st.close()
    nc.all_engine_barrier()
    for s_ in list(P.esem.values()) + SEMS:
        nc.gpsimd.sem_clear(s_)
    nc.all_engine_barrier()
    return nc


_NC = None


def _prep(inputs):
    f = lambda k: np.ascontiguousarray(np.asarray(inputs[k], dtype=np.float32))
    x = f("x")
    vecs = np.zeros((128, NV), np.float32)
    pc = lambda v: np.asarray(v, np.float32).reshape(8, 128).T
    vecs[:, 0:8] = pc(inputs["norm_mix_g"][0])
    for k in range(4):
        vecs[:, 8 + k * 8:16 + k * 8] = pc(inputs["conv_w"][0, k])
    vecs[:, 40:48] = pc(inputs["conv_b"][0])
    for d in range(2):
        vecs[:, 48 + d * 8:56 + d * 8] = pc(inputs["lru_lambda"][0, d])
        vecs[:, 64 + d * 8:72 + d * 8] = pc(inputs["lru_ba"][0, d])
        vecs[:, 80 + d * 8:88 + d * 8] = pc(inputs["lru_bx"][0, d])
    vecs[:, 96:112] = np.asarray(inputs["b_gate"][0], np.float32).reshape(16, 128).T
    vecs[:, 112:120] = pc(inputs["norm_ffn_g"][0])
    wbd = np.zeros((4, 128, 8, 128), np.float32)
    for d in range(2):
        for gi, nm in enumerate(("lru_wa", "lru_wx")):
            W = np.asarray(inputs[nm][0, d], np.float32)
            for c in range(8):
                wbd[d * 2 + gi, 0:64, c, 0:64] = W[2 * c]
                wbd[d * 2 + gi, 64:128, c, 64:128] = W[2 * c + 1]
    sink = np.asarray(inputs["attn_sink"][0], np.float32)
    order = []
    for g in range(4):
        order += [4 * g, 4 * g + 2, 4 * g + 1, 4 * g + 3]
    sink_r = sink[order].reshape(1, 16)
    common = {"w_in": f("w_in")[0], "w_out": f("w_out")[0], "w_f1": f("w_ffn_in")[0], "w_f2": f("w_ffn_out")[0],
              "wbd": wbd, "vecs": vecs, "gfin": f("norm_final_g").reshape(1, D), "sink": np.ascontiguousarray(sink_r)}
    maps = []
    S = x.shape[1]
    cw = np.asarray(inputs["conv_w"][0], np.float32)
    zero_c = np.zeros((1024,), np.float32)
    taps = {0: [cw[0], cw[1], cw[2], cw[3], zero_c],
            1: [cw[3], cw[2], cw[1], cw[0], zero_c]}
    wbd_dir = {d: (wbd[d * 2 + 0], wbd[d * 2 + 1]) for d in range(2)}
    for core in range(8):
        b, q = core // 4, core % 4
        xe = np.zeros((TH, D), np.float32)
        lo = q * T - 128
        hi = (q + 1) * T + 128
        slo, shi = max(lo, 0), min(hi, S)
        xe[slo - lo:shi - lo] = x[b, slo:shi]
        fl = np.zeros((128, NFLAG), np.float32)
        fl[:, 0] = 1.0 if q > 0 else 0.0
        fl[:, 1] = 1.0 if q < 3 else 0.0
        xsl = np.zeros((NS, SR, D), np.float32)
        sv = np.zeros((128, NS * SV), np.float32)
        swbd = np.zeros((NS * 2, 128, 8, 128), np.float32)
        for s_i in range(NS):
            if s_i < q:
                d, chunk, chain = 0, s_i, (1.0 if s_i > 0 else 0.0)
                tok = np.arange(chunk * T - 2, (chunk + 1) * T + 2)
            else:
                j = s_i - q
                d, chunk, chain = 1, 3 - j, (1.0 if j > 0 else 0.0)
                tok = np.arange((chunk + 1) * T, (chunk + 1) * T - (T + 4), -1)
            ok = (tok >= 0) & (tok < S)
            rows = np.zeros((T + 4, D), np.float32)
            rows[ok] = x[b, tok[ok]]
            xsl[s_i, 0:T + 4] = rows
            fl[:, 2 + s_i] = chain
            vb = s_i * SV
            for k in range(5):
                sv[:, vb + k * 8:vb + k * 8 + 8] = pc(taps[d][k])
            sv[:, vb + 40:vb + 48] = pc(inputs["conv_b"][0])
            sv[:, vb + 48:vb + 56] = pc(inputs["lru_lambda"][0, d])
            sv[:, vb + 56:vb + 64] = pc(inputs["lru_ba"][0, d])
            sv[:, vb + 64:vb + 72] = pc(inputs["lru_bx"][0, d])
            swbd[s_i * 2 + 0] = wbd_dir[d][0]
            swbd[s_i * 2 + 1] = wbd_dir[d][1]
        if q >= 1:
            fl[:, 5 + (q - 1)] = 1.0
        if q <= 2:
            fl[:, 8 + 2] = 1.0
        m = dict(common)
        m["x_ext"] = xe
        m["flags"] = fl
        m["xslot"] = xsl
        m["svecs"] = sv
        m["swbd"] = swbd
        maps.append(m)
    return maps


def kernel(**inputs):
    global _NC
    if _NC is None:
        _NC = build_program()
    maps = _prep(inputs)
    res = run_bass_kernel_spmd(_NC, maps, core_ids=list(range(8)))
    out = np.zeros((2, 4 * T, D), np.float32)
    for core in range(8):
        b, q = core // 4, core % 4
        out[b, q * T:(q + 1) * T] = np.asarray(res.results[core]["out"], dtype=np.float32)
    return out
```

```python
import numpy as np
import concourse.bass as bass
import concourse.mybir as mybir
from concourse.bass_utils import run_bass_kernel_spmd
from contextlib import ExitStack

F32 = mybir.dt.float32
BF16 = mybir.dt.bfloat16
I32 = mybir.dt.int32
AF = mybir.ActivationFunctionType
ALU = mybir.AluOpType

D = 1024
T = 2048
TH = 2304
NT = 18
KC = 8
FC = 22
NV = 120
NFLAG = 11
NS = 3
SR = 2176
SV = 72
SB_BASE = 16640
SB_END = 229376
EPS = 1e-6


class Prog:
    COMPUTE = ("pe", "act", "dve", "pool")

    def __init__(self, nc):
        self.nc = nc
        self.ops = []
        self.lastw = {}
        self.readers = {}
        self.pending = {}
        self.since_barrier = []

    def barrier(self):
        last = {}
        for i in self.since_barrier:
            op = self.ops[i]
            if op["dsem"] is not None:
                last[("d", id(op["dsem"]))] = i
            else:
                last[("e", op["eng"])] = i
        deps = set(last.values())
        for e in ("pe", "act", "dve", "pool", "sp"):
            self.pending.setdefault(e, set()).update(deps)
        self.since_barrier = []

    def add(self, eng, fn, r=(), w=(), dsem=None, ninc=16):
        idx = len(self.ops)
        deps = set()
        rawdeps = set()
        for k in r:
            if k in self.lastw:
                deps.add(self.lastw[k])
                rawdeps.add(self.lastw[k])
        for k in w:
            if k in self.lastw:
                deps.add(self.lastw[k])
            for x in self.readers.get(k, ()):
                deps.add(x)
        keep = []
        for d in deps:
            od = self.ops[d]
            same = (od["eng"] == eng and od["dsem"] is None and dsem is None)
            if same:
                if eng == "pe":
                    continue
                if d not in rawdeps:
                    continue
            keep.append(d)
        pend = self.pending.pop(eng, None)
        if pend:
            for d in pend:
                od = self.ops[d]
                if od["eng"] == eng and od["dsem"] is None:
                    continue
                if d not in keep:
                    keep.append(d)
        op = dict(eng=eng, fn=fn, deps=keep, dsem=dsem, ninc=ninc, inc=False, idx=idx)
        self.ops.append(op)
        self.since_barrier.append(idx)
        for d in keep:
            self.ops[d]["inc"] = True
        for k in w:
            self.lastw[k] = idx
            self.readers[k] = []
        for k in r:
            if k not in w:
                self.readers.setdefault(k, []).append(idx)
        return idx

    def emit(self, st, final_waits=()):
        nc = self.nc
        esem = self.esem
        cnt = {}
        for op in self.ops:
            if op["dsem"] is not None:
                s = op["dsem"]
                cnt[id(s)] = cnt.get(id(s), 0) + op["ninc"]
                op["csem"] = s
                op["cval"] = cnt[id(s)]
            elif op["inc"]:
                s = esem[op["eng"]]
                cnt[id(s)] = cnt.get(id(s), 0) + 1
                op["csem"] = s
                op["cval"] = cnt[id(s)]
            else:
                op["csem"] = None
        queues = {}
        for op in self.ops:
            queues.setdefault(op["eng"], []).append(op)
        ops = self.ops
        block = st.enter_context(nc.Block())

        def run(engname, eng):
            waited = {}
            for op in queues.get(engname, []):
                need = {}
                for d in op["deps"]:
                    od = ops[d]
                    s = od["csem"]
                    k = id(s)
                    if k not in need or need[k][1] < od["cval"]:
                        need[k] = (s, od["cval"])
                for k, (s, v) in need.items():
                    if waited.get(k, 0) >= v:
                        continue
                    eng.wait_ge(s, v)
                    waited[k] = v
                ins = op["fn"](eng)
                if op["dsem"] is None and op["inc"]:
                    ins.then_inc(op["csem"], 1)
            if engname == "sp":
                for d in final_waits:
                    od = ops[d]
                    eng.wait_ge(od["csem"], od["cval"])

        @block.sync
        def _(e):
            run("sp", e)

        @block.tensor
        def _(e):
            run("pe", e)

        @block.scalar
        def _(e):
            run("act", e)

        @block.vector
        def _(e):
            run("dve", e)

        @block.gpsimd
        def _(e):
            run("pool", e)


class Arena:
    def __init__(self, nc):
        self.nc = nc
        self.off = SB_BASE
        self.limit = SB_END
        self.n = 0

    def seek(self, off, limit=SB_END):
        self.off = off
        self.limit = limit

    def alloc(self, shape, dt):
        esz = 4 if dt in (F32, I32) else 2
        nb = int(np.prod(shape[1:])) * esz
        nb = (nb + 63) // 64 * 64
        assert self.off + nb <= self.limit, ("SBUF overflow", self.off, nb, self.limit)
        self.n += 1
        h = self.nc.alloc_sbuf_tensor_at("t%d" % self.n, list(shape), dt, offset=self.off)
        self.off += nb
        return h


def build_program(debug=False, stop=None):
    nc = bass.Bass("TRN2", target_bir_lowering=False)

    def din(name, shape, dt=F32):
        return nc.dram_tensor(name, list(shape), dt, kind="ExternalInput").ap()

    x_ext = din("x_ext", [TH, D])
    w_in = din("w_in", [D, 5632])
    w_out = din("w_out", [D, D])
    w_f1 = din("w_f1", [D, 5632])
    w_f2 = din("w_f2", [2816, D])
    wbd = din("wbd", [4, 128, 8, 128])
    vecs_d = din("vecs", [128, NV])
    gfin_d = din("gfin", [1, D])
    sink_d = din("sink", [1, 16])
    flags_d = din("flags", [128, NFLAG])
    xslot = din("xslot", [NS, SR, D])
    svecs_d = din("svecs", [128, NS * SV])
    swbd = din("swbd", [NS * 2, 128, 8, 128])
    out_d = nc.dram_tensor("out", [T, D], F32, kind="ExternalOutput").ap()

    P = Prog(nc)
    A = Arena(nc)
    st = ExitStack()
    nsem = [0]

    P.esem = {e: nc.alloc_semaphore(name="s_" + e) for e in Prog.COMPUTE}
    SEMS = [nc.alloc_semaphore(name="d%d" % i) for i in range(12)]
    for s_ in list(P.esem.values()) + SEMS:
        nc.gpsimd.sem_clear(s_)
    nc.all_engine_barrier()
    HWL = [1, 2, 3, 4, 9, 10]
    SWL = [5, 6, 7, 8]
    hwptr = [0]
    swptr = [0]

    def newsem(kind="hw"):
        if kind == "hw":
            s_ = SEMS[HWL[hwptr[0]]]
            hwptr[0] += 1
        else:
            s_ = SEMS[SWL[swptr[0]]]
            swptr[0] += 1
        return s_

    def phase_sems():
        hwptr[0] = 0
        swptr[0] = 0

    ps = [nc.alloc_psum_tensor("ps%d" % i, [128, 512], F32) for i in range(8)]

    def dma(q, out, in_, sem, r=(), w=()):
        return P.add(q, lambda e: e.dma_start(out=out, in_=in_).then_inc(sem, 16), r=r, w=w, dsem=sem)

    def dma_multi(q, pairs, sem, r=(), w=()):
        def fn(e):
            for (o, i) in pairs:
                ins = e.dma_start(out=o, in_=i).then_inc(sem, 16)
            return ins
        return P.add(q, fn, r=r, w=w, dsem=sem, ninc=16 * len(pairs))

    def K(b):
        return [(b, i) for i in range(4)]

    def finish_early(src_ap):
        sem_f = SEMS[11]
        P.barrier()
        f = dma("sp", out_d[0:128, 0:src_ap.shape[-1]], src_ap, sem_f)
        P.emit(st, final_waits=[f])
        st.close()
        nc.all_engine_barrier()
        for s_ in list(P.esem.values()) + SEMS:
            nc.gpsimd.sem_clear(s_)
        nc.all_engine_barrier()
        return nc

    ident = A.alloc([128, 128], BF16)
    ones = A.alloc([128, 128], BF16)
    onesLR = A.alloc([128, 2, 128], BF16)
    vecs = A.alloc([128, NV], F32)
    flags = A.alloc([128, NFLAG], F32)
    clv = A.alloc([128, 32], F32)
    es_row = A.alloc([1, 4, 4, 128], BF16)
    sinkt = A.alloc([1, 32], F32)
    svecs = A.alloc([128, NS * SV], F32)
    scl = A.alloc([128, 48], F32)
    fstate = A.alloc([128, NS + 1, 8], F32)
    sinit = A.alloc([128, NS, 8], F32)
    hinv = A.alloc([128, 2, 8], F32)
    ss = A.alloc([128, 128], F32)
    rt = A.alloc([128, 128], F32)
    rstd = A.alloc([128, 128], F32)
    iota_i = A.alloc([128, 384], I32)
    identf = A.alloc([128, 128], F32)
    sem_c = SEMS[0]
    C_END = A.off
    PA_OFF = C_END
    XNT_OFF = PA_OFF + 32768
    YBG_OFF = XNT_OFF + 36864
    X2_OFF = YBG_OFF + 32768
    X2_END = X2_OFF + 65536

    G1C, CW, CB, LAM, BA, BX, BG, G2C = 0, 8, 40, 48, 64, 80, 96, 112

    dma_multi("sp", [(vecs[:], vecs_d), (flags[:], flags_d), (sinkt[:, 0:16], sink_d), (svecs[:], svecs_d)], sem_c, w=["vecs", "flags", "sinkt", "svecs"])
    P.add("pool", lambda e: e.memset(fstate[:], 0.0), w=["finals0"])
    for s_i in range(NS):
        P.add("act", lambda e, s_i=s_i: e.activation(out=scl[:, s_i * 8:s_i * 8 + 8], in_=svecs[:, s_i * SV + 48:s_i * SV + 56], func=AF.Exp, scale=-1.0), r=["svecs"], w=[("scl0", s_i)])
        P.add("act", lambda e, s_i=s_i: e.activation(out=scl[:, 24 + s_i * 8:24 + s_i * 8 + 8], in_=scl[:, s_i * 8:s_i * 8 + 8], func=AF.Ln, bias=1.0), r=[("scl0", s_i)], w=[("scl1", s_i)])
        P.add("dve", lambda e, s_i=s_i: e.tensor_scalar(out=scl[:, s_i * 8:s_i * 8 + 8], in0=scl[:, 24 + s_i * 8:24 + s_i * 8 + 8], scalar1=-8.0, scalar2=None, op0=ALU.mult), r=[("scl1", s_i)], w=[("scl2", s_i)])
        P.add("dve", lambda e, s_i=s_i: e.tensor_scalar(out=scl[:, 24 + s_i * 8:24 + s_i * 8 + 8], in0=scl[:, 24 + s_i * 8:24 + s_i * 8 + 8], scalar1=-16.0, scalar2=None, op0=ALU.mult), r=[("scl1", s_i), ("scl2", s_i)], w=[("scl", s_i)])
    P.add("pool", lambda e: e.memset(ones[:], 1.0), w=["ones"])
    P.add("pool", lambda e: e.memset(ss[:], 0.0), w=["ss"])
    P.add("pool", lambda e: e.iota(iota_i[:, 0:128], pattern=[[1, 128]], base=0, channel_multiplier=-1), w=["iota"])
    P.add("dve", lambda e: e.tensor_copy(out=identf[:], in_=iota_i[:, 0:128]), r=["iota"], w=["identf"])
    P.add("dve", lambda e: e.tensor_scalar(out=ident[:], in0=identf[:], scalar1=0.0, scalar2=None, op0=ALU.is_equal),
          r=["identf"], w=["ident"])
    P.add("dve", lambda e: e.tensor_tensor(out=onesLR[:], in0=ones[:].unsqueeze(1).broadcast_to([128, 2, 128]),
                                           in1=flags[:, 0:2].unsqueeze(2).broadcast_to([128, 2, 128]), op=ALU.mult),
          r=["ones", "flags"], w=["onesLR"])
    P.add("act", lambda e: e.activation(out=clv[:, 0:16], in_=vecs[:, LAM:LAM + 16], func=AF.Exp, scale=-1.0), r=["vecs"], w=["cl0"])
    P.add("act", lambda e: e.activation(out=clv[:, 16:32], in_=clv[:, 0:16], func=AF.Ln, bias=1.0), r=["cl0"], w=["cl1"])
    P.add("dve", lambda e: e.tensor_scalar(out=clv[:, 0:16], in0=clv[:, 16:32], scalar1=-8.0, scalar2=None, op0=ALU.mult), r=["cl1"], w=["cl2"])
    P.add("dve", lambda e: e.tensor_scalar(out=clv[:, 16:32], in0=clv[:, 16:32], scalar1=-16.0, scalar2=None, op0=ALU.mult), r=["cl1", "cl2"], w=["cl"])
    P.add("act", lambda e: e.activation(out=sinkt[:, 16:32], in_=sinkt[:, 0:16], func=AF.Exp), r=["sinkt"], w=["esink"])
    P.add("dve", lambda e: e.tensor_copy(out=es_row[:], in_=sinkt[:, 16:32].rearrange("p (g h) -> p g h", g=4).unsqueeze(3).broadcast_to([1, 4, 4, 128])),
          r=["esink"], w=["es_row"])

    A.seek(PA_OFF)
    Pa = A.alloc([128, KC, T], BF16)
    xnT = A.alloc([128, KC, TH], BF16)
    assert A.off == YBG_OFF
    W_OFF = A.off

    psrot = [0]

    def nextbank(lo=0, hi=8):
        b = lo + psrot[0] % (hi - lo)
        psrot[0] += 1
        return b

    def norm_a(src_ap, src_keys, sscol, sq_t, sq_key):
        P.add("act", lambda e: e.activation(out=sq_t[:], in_=src_ap, func=AF.Square, accum_out=ss[:, sscol:sscol + 1]),
              r=list(src_keys) + ["ss"], w=[("ssc", sscol), sq_key])
        P.add("act", lambda e: e.activation(out=rt[:, sscol:sscol + 1], in_=ss[:, sscol:sscol + 1], func=AF.Sqrt, scale=1.0 / D, bias=EPS),
              r=[("ssc", sscol)], w=[("rt", sscol)])
        P.add("dve", lambda e: e.reciprocal(out=rstd[:, sscol:sscol + 1], in_=rt[:, sscol:sscol + 1]), r=[("rt", sscol)], w=[("rstd", sscol)])

    def norm_b(src_ap, src_keys, sscol, xs_t, xs_key, gcol, dstT, dcol0, dkey):
        P.add("act", lambda e: e.mul(out=xs_t[:], in_=src_ap, mul=rstd[:, sscol:sscol + 1]),
              r=list(src_keys) + [("rstd", sscol)], w=[xs_key])
        for hb in range(2):
            bk = nextbank()

            def fn(e, hb=hb, bk=bk):
                for k in range(4):
                    c = hb * 4 + k
                    ins = e.matmul(ps[bk][:, k * 128:(k + 1) * 128], lhsT=xs_t[:, c * 128:(c + 1) * 128], rhs=ident[:], start=True, stop=True)
                return ins
            P.add("pe", fn, r=[xs_key, "ident"], w=[("ps", bk)])
            P.add("dve", lambda e, hb=hb, bk=bk: e.tensor_tensor(
                out=dstT[:, hb * 4:(hb + 1) * 4, dcol0:dcol0 + 128],
                in0=ps[bk][:, :].rearrange("p (a b) -> p a b", a=4),
                in1=vecs[:, gcol + hb * 4:gcol + hb * 4 + 4].unsqueeze(2).broadcast_to([128, 4, 128]), op=ALU.mult),
                r=[("ps", bk), "vecs"], w=[(dkey, dcol0, hb)])

    def proj_fm(wt_ap_fn, wkey, rhsT, col0, n, evac):
        bk = nextbank()

        def fn(e):
            for kc in range(KC):
                ins = e.matmul(ps[bk][:, 0:n], lhsT=wt_ap_fn(kc), rhs=rhsT[:, kc, col0:col0 + n], start=(kc == 0), stop=(kc == KC - 1))
            return ins
        P.add("pe", fn, r=[wkey], w=[("ps", bk)])
        evac(bk)


    A.seek(XNT_OFF, YBG_OFF)
    xsT = A.alloc([128, KC, SR], BF16)
    A.seek(W_OFF)
    S_XT = [A.alloc([128, D], F32) for _ in range(6)]
    S_XS = [A.alloc([128, D], BF16) for _ in range(2)]
    S_SQJ = A.alloc([128, D], BF16)
    wlu = [A.alloc([128, KC, 128], BF16) for _ in range(2)]
    swbd_s = A.alloc([128, NS * 2, 8, 128], BF16)
    S_UT = [A.alloc([128, 2052], F32) for _ in range(2)]
    S_BF = [[A.alloc([128, T], F32) for _ in range(4)] for _ in range(2)]
    A.seek(PA_OFF, XNT_OFF)
    S_UC = [A.alloc([128, T], F32) for _ in range(2)]
    S_UCB = [A.alloc([128, T], BF16) for _ in range(2)]
    phase_sems()
    sem_x = [newsem() for _ in range(6)]
    sem_wlu = [newsem('sw') for _ in range(2)]
    sem_sbd = newsem('sw')
    w_in_v = w_in.rearrange("(kc p) n -> p kc n", p=128)
    dma_multi("pool", [(swbd_s[:, i, :, :], swbd[i]) for i in range(NS * 2)], sem_sbd, w=["swbd"])
    SUB = ((0, 512), (512, 512), (1024, 512), (1536, 512), (2048, 4))
    slot_rot = [0]

    def KP(name, par):
        return [((name, par), i) for i in range(4)]

    def slot_s1(s_i, c):
        sl = c % 2
        dma("pool", wlu[sl][:], w_in_v[:, :, c * 128:(c + 1) * 128], sem_wlu[sl], w=[("wlu", sl)])
        u_t = S_UT[sl]
        uc_, ucb_ = S_UC[sl], S_UCB[sl]
        for bi, (j0, n) in enumerate(SUB):
            def ev(bk, j0=j0, n=n, bi=bi):
                P.add("act", lambda e: e.copy(out=u_t[:, j0:j0 + n], in_=ps[bk][:, 0:n]), r=[("ps", bk)], w=[("suT", sl, j0)])
            proj_fm(lambda kc: wlu[sl][:, kc, :], ("wlu", sl), xsT, j0, n, ev)
        ukeys = [("suT", sl, j0) for (j0, n) in SUB]
        vb = s_i * SV
        cwa = [svecs[:, vb + k * 8 + c:vb + k * 8 + c + 1] for k in range(5)]
        cba = svecs[:, vb + 40 + c:vb + 40 + c + 1]
        P.add("dve", lambda e: e.tensor_scalar(out=uc_[:], in0=u_t[:, 0:T], scalar1=cwa[0], scalar2=cba, op0=ALU.mult, op1=ALU.add),
              r=ukeys + ["svecs"], w=[("suc", sl)])
        for k in (1, 2, 3):
            P.add("dve", lambda e, k=k: e.scalar_tensor_tensor(out=uc_[:], in0=u_t[:, k:k + T], scalar=cwa[k], in1=uc_[:], op0=ALU.mult, op1=ALU.add),
                  r=ukeys + [("suc", sl)], w=[("suc", sl)])

    def slot_s1b(s_i, c):
        sl = c % 2
        uc_, ucb_ = S_UC[sl], S_UCB[sl]
        P.add("act", lambda e: e.copy(out=ucb_[:], in_=uc_[:]), r=[("suc", sl)], w=[("sucb", sl)])

    def slot_s2(s_i, c):
        sl = c % 2
        uc_, ucb_ = S_UC[sl], S_UCB[sl]
        rb, ib, ab, hb_ = S_BF[sl]
        vb = s_i * SV
        for gate, dst, boff, nm in ((0, rb, 56, "B0"), (1, ib, 64, "B1")):
            bias_ap = svecs[:, vb + boff + c:vb + boff + c + 1]
            for tb in range(4):
                bk = nextbank()
                P.add("pe", lambda e, bk=bk, gate=gate, tb=tb: e.matmul(ps[bk][:, :], lhsT=swbd_s[:, s_i * 2 + gate, c, :], rhs=ucb_[:, tb * 512:(tb + 1) * 512], start=True, stop=True),
                      r=["swbd", ("sucb", sl)], w=[("ps", bk)])
                P.add("act", lambda e, bk=bk, dst=dst, tb=tb, bias_ap=bias_ap: e.activation(out=dst[:, tb * 512:(tb + 1) * 512], in_=ps[bk][:, :], func=AF.Sigmoid, bias=bias_ap),
                      r=[("ps", bk), "svecs"], w=[((nm, sl), tb)])
        ci = s_i * 8 + c
        P.add("act", lambda e: e.activation(out=ab[:], in_=rb[:], func=AF.Exp, scale=scl[:, ci:ci + 1]), r=KP("B0", sl) + [("scl", s_i)], w=KP("B2", sl))
        P.add("act", lambda e: e.activation(out=rb[:], in_=rb[:], func=AF.Exp, scale=scl[:, 24 + ci:24 + ci + 1]), r=KP("B0", sl) + [("scl", s_i)], w=KP("B0", sl))
        P.add("act", lambda e: e.activation(out=rb[:], in_=rb[:], func=AF.Sqrt, scale=-1.0, bias=1.0), r=KP("B0", sl), w=KP("B0", sl))
        P.add("dve", lambda e: e.tensor_tensor(out=ib[:], in0=rb[:], in1=ib[:], op=ALU.mult), r=KP("B0", sl) + KP("B1", sl), w=KP("B1", sl))
        P.add("dve", lambda e: e.tensor_tensor(out=ib[:], in0=ib[:], in1=uc_[:], op=ALU.mult), r=KP("B1", sl) + [("suc", sl)], w=KP("B1", sl))
        P.add("dve", lambda e: e.tensor_tensor_scan(out=hb_[:, :], data0=ab[:, :], data1=ib[:, :], initial=sinit[:, s_i, c:c + 1], op0=ALU.mult, op1=ALU.add),
              r=KP("B2", sl) + KP("B1", sl) + [("sinit", s_i)], w=KP("B3", sl))
        P.add("dve", lambda e: e.tensor_copy(out=fstate[:, s_i + 1, c:c + 1], in_=hb_[:, T - 1:T]), r=KP("B3", sl) + ["finals0"], w=[("fin", s_i, c)])

    for s_i in range(NS):
        nti = SR // 128
        for i in range(nti + 1):
            if i < nti:
                s3 = i % 6
                dma("sp", S_XT[s3][:], xslot[s_i, i * 128:(i + 1) * 128, :], sem_x[s3], w=[("xt", s3)])
                norm_a(S_XT[s3][:], [("xt", s3)], 64 + (s_i * 17 + i) % 64, S_SQJ, "sqj")
            if i >= 1:
                j = i - 1
                norm_b(S_XT[j % 6][:], [("xt", j % 6)], 64 + (s_i * 17 + j) % 64, S_XS[j % 2], ("xs", j % 2), G1C, xsT, j * 128, "xsT")
        P.add("dve", lambda e, s_i=s_i: e.tensor_scalar(out=sinit[:, s_i, :], in0=fstate[:, s_i, :], scalar1=flags[:, 2 + s_i:3 + s_i], scalar2=None, op0=ALU.mult),
              r=["flags", "finals0"] + [("fin", s_i - 1, c) for c in range(KC) if s_i > 0], w=[("sinit", s_i)])
        P.barrier()
        slot_s1(s_i, 0)
        slot_s1b(s_i, 0)
        for c in range(KC):
            if c + 1 < KC:
                slot_s1(s_i, c + 1)
            slot_s2(s_i, c)
            if c + 1 < KC:
                slot_s1b(s_i, c + 1)
        P.barrier()
    for d_ in range(2):
        fc = 5 + 3 * d_
        P.add("dve", lambda e, d_=d_, fc=fc: e.tensor_scalar(out=hinv[:, d_, :], in0=fstate[:, 1, :], scalar1=flags[:, fc:fc + 1], scalar2=None, op0=ALU.mult),
              r=["flags"], w=[("hinv", d_)])
        for s_i in (1, 2):
            P.add("dve", lambda e, d_=d_, fc=fc, s_i=s_i: e.scalar_tensor_tensor(out=hinv[:, d_, :], in0=fstate[:, s_i + 1, :], scalar=flags[:, fc + s_i:fc + s_i + 1], in1=hinv[:, d_, :], op0=ALU.mult, op1=ALU.add),
                  r=["flags", ("hinv", d_)], w=[("hinv", d_)])
    P.barrier()
    if stop == 'S':
        return finish_early(hinv[:].rearrange('p a b -> p (a b)'))

    A.seek(W_OFF)
    xt = [A.alloc([128, D], F32) for _ in range(6)]
    xs = [A.alloc([128, D], BF16) for _ in range(2)]
    sqj = A.alloc([128, D], BF16)
    phase_sems()
    sem_x = [newsem() for _ in range(6)]
    for i in range(NT):
        s3 = i % 6
        dma("sp", xt[s3][:], x_ext[i * 128:(i + 1) * 128, :], sem_x[s3], w=[("xt", s3)])
        norm_a(xt[s3][:], [("xt", s3)], i, sqj, "sqj")
        if i >= 1:
            norm_b(xt[(i - 1) % 6][:], [("xt", (i - 1) % 6)], i - 1, xs[(i - 1) % 2], ("xs", (i - 1) % 2), G1C, xnT, (i - 1) * 128, "xnT")
    norm_b(xt[(NT - 1) % 6][:], [("xt", (NT - 1) % 6)], NT - 1, xs[(NT - 1) % 2], ("xs", (NT - 1) % 2), G1C, xnT, (NT - 1) * 128, "xnT")
    P.barrier()
    if stop == 'A':
        return finish_early(xnT[:, 0:4, 0:128].bitcast(F32) if False else xt[0][:])

    A.seek(W_OFF)
    wl = [A.alloc([128, KC, 3, 128], BF16) for _ in range(2)]
    wbd_s = A.alloc([128, 4, 8, 128], BF16)
    uT = [A.alloc([128, 2052], F32) for _ in range(2)]
    uc2 = [A.alloc([128, T], F32) for _ in range(2)]
    ucb2 = [A.alloc([128, T], BF16) for _ in range(2)]
    G12 = [A.alloc([128, T], BF16) for _ in range(2)]
    gel = A.alloc([128, T], BF16)
    gA = A.alloc([128, T], BF16)
    Bf = [A.alloc([128, T], F32) for _ in range(6)]
    phase_sems()
    sem_wl = [newsem('sw') for _ in range(2)]
    sem_bd = newsem('sw')
    w_in_v = w_in.rearrange("(kc p) n -> p kc n", p=128)

    dma_multi("pool", [(wbd_s[:, i, :, :], wbd[i]) for i in range(4)], sem_bd, w=["wbd"])

    UB = ((0, 512), (512, 512), (1024, 512), (1536, 512), (2048, 3))

    def lru_dir(c, d):
        par = c % 2
        uc, ucb = uc2[par], ucb2[par]
        rb, ib, ab = Bf[3 * d], Bf[3 * d + 1], Bf[3 * d + 2]
        kr, ki, ka = "B%d" % (3 * d), "B%d" % (3 * d + 1), "B%d" % (3 * d + 2)
        for gate, dst, bcol, nm in ((0, rb, BA, kr), (1, ib, BX, ki)):
            bias_ap = vecs[:, bcol + d * 8 + c:bcol + d * 8 + c + 1]
            for tb in range(4):
                bk = nextbank()
                P.add("pe", lambda e, bk=bk, gate=gate, tb=tb: e.matmul(ps[bk][:, :], lhsT=wbd_s[:, d * 2 + gate, c, :], rhs=ucb[:, tb * 512:(tb + 1) * 512], start=True, stop=True),
                      r=["wbd", ("ucb", par)], w=[("ps", bk)])
                P.add("act", lambda e, bk=bk, dst=dst, tb=tb, bias_ap=bias_ap: e.activation(out=dst[:, tb * 512:(tb + 1) * 512], in_=ps[bk][:, :], func=AF.Sigmoid, bias=bias_ap),
                      r=[("ps", bk), "vecs"], w=[(nm, tb)])
        ci = d * 8 + c
        P.add("act", lambda e: e.activation(out=ab[:], in_=rb[:], func=AF.Exp, scale=clv[:, ci:ci + 1]), r=K(kr) + ["cl"], w=K(ka))
        P.add("act", lambda e: e.activation(out=rb[:], in_=rb[:], func=AF.Exp, scale=clv[:, 16 + ci:16 + ci + 1]), r=K(kr) + ["cl"], w=K(kr))
        P.add("act", lambda e: e.activation(out=rb[:], in_=rb[:], func=AF.Sqrt, scale=-1.0, bias=1.0), r=K(kr), w=K(kr))
        P.add("dve", lambda e: e.tensor_tensor(out=ib[:], in0=rb[:], in1=ib[:], op=ALU.mult), r=K(kr) + K(ki), w=K(ki))
        P.add("dve", lambda e: e.tensor_tensor(out=ib[:], in0=ib[:], in1=uc[:], op=ALU.mult), r=K(ki) + [("uc", par)], w=K(ki))
        rev = (lambda ap: ap[:, ::-1]) if d == 1 else (lambda ap: ap[:, :])
        e0 = 0 if d == 0 else T - 1
        P.add("dve", lambda e: e.scalar_tensor_tensor(out=ib[:, e0:e0 + 1], in0=ab[:, e0:e0 + 1], scalar=hinv[:, d, c:c + 1], in1=ib[:, e0:e0 + 1], op0=ALU.mult, op1=ALU.add),
              r=K(ka) + K(ki) + [("hinv", d)], w=K(ki))
        P.add("dve", lambda e: e.tensor_tensor_scan(out=rev(rb), data0=rev(ab), data1=rev(ib), initial=0.0, op0=ALU.mult, op1=ALU.add),
              r=K(ka) + K(ki) + K(kr), w=K(kr))

    def lru_s1(c):
        sl = c % 2
        uc = uc2[sl]
        cols = [c * 128, 1024 + c * 128, 3584 + c * 128]
        dma_multi("pool", [(wl[sl][:, :, j, :], w_in_v[:, :, cols[j]:cols[j] + 128]) for j in range(3)], sem_wl[sl], w=[("wl", sl)])
        u_t = uT[sl]
        for bi, (j0, n) in enumerate(UB):
            def ev(bk, j0=j0, n=n, bi=bi):
                if bi % 2 == 0:
                    P.add("act", lambda e: e.copy(out=u_t[:, j0:j0 + n], in_=ps[bk][:, 0:n]), r=[("ps", bk)], w=[("uT", sl, j0)])
                else:
                    P.add("dve", lambda e: e.tensor_copy(out=u_t[:, j0:j0 + n], in_=ps[bk][:, 0:n]), r=[("ps", bk)], w=[("uT", sl, j0)])
            proj_fm(lambda kc: wl[sl][:, kc, 0, :], ("wl", sl), xnT, 126 + j0, n, ev)
        ukeys = [("uT", sl, j0) for (j0, n) in UB]
        cwa = [vecs[:, CW + k * 8 + c:CW + k * 8 + c + 1] for k in range(4)]
        cba = vecs[:, CB + c:CB + c + 1]
        P.add("dve", lambda e: e.tensor_scalar(out=uc[:], in0=u_t[:, 0:T], scalar1=cwa[0], scalar2=cba, op0=ALU.mult, op1=ALU.add),
              r=ukeys + ["vecs"], w=[("uc", sl)])
        for k in (1, 2, 3):
            P.add("dve", lambda e, k=k: e.scalar_tensor_tensor(out=uc[:], in0=u_t[:, k:k + T], scalar=cwa[k], in1=uc[:], op0=ALU.mult, op1=ALU.add),
                  r=ukeys + [("uc", sl)], w=[("uc", sl)])
        for tb in range(4):
            def evg(bk, tb=tb):
                P.add("act", lambda e: e.activation(out=gel[:, tb * 512:(tb + 1) * 512], in_=ps[bk][:, :], func=AF.Gelu_apprx_tanh), r=[("ps", bk)], w=[("gel", tb)])
            proj_fm(lambda kc: wl[sl][:, kc, 1, :], ("wl", sl), xnT, 128 + tb * 512, 512, evg)
        bga = vecs[:, BG + c:BG + c + 1]
        for tb in range(4):
            def evz(bk, tb=tb):
                P.add("act", lambda e: e.activation(out=gA[:, tb * 512:(tb + 1) * 512], in_=ps[bk][:, :], func=AF.Sigmoid, bias=bga),
                      r=[("ps", bk), "vecs"], w=[("gA", tb)])
            proj_fm(lambda kc: wl[sl][:, kc, 2, :], ("wl", sl), xnT, 128 + tb * 512, 512, evz)
        P.add("dve", lambda e: e.tensor_tensor(out=G12[sl][:], in0=gel[:], in1=gA[:], op=ALU.mult), r=K("gel") + K("gA"), w=[("G1", sl)])

    def lru_s1b(c):
        sl = c % 2
        P.add("act", lambda e: e.copy(out=ucb2[sl][:], in_=uc2[sl][:]), r=[("uc", sl)], w=[("ucb", sl)])

    def lru_s2(c):
        sl = c % 2
        lru_dir(c, 0)
        lru_dir(c, 1)
        P.add("dve", lambda e: e.tensor_tensor(out=Bf[0][:], in0=Bf[0][:], in1=Bf[3][:], op=ALU.add), r=K("B0") + K("B3"), w=K("B0"))
        P.add("dve", lambda e: e.tensor_tensor(out=Pa[:, c, :], in0=Bf[0][:], in1=G12[sl][:], op=ALU.mult), r=K("B0") + [("G1", sl)], w=[("Pa", c)])

    lru_s1(0)
    lru_s1b(0)
    for c in range(KC):
        if c + 1 < KC:
            lru_s1(c + 1)
        lru_s2(c)
        if c + 1 < KC:
            lru_s1b(c + 1)
    if stop in ('B', 'B1'):
        return finish_early(Bf[3][:, 0:1024])
    P.barrier()
    if stop == 'X':
        return finish_early(Bf[3][:, 0:1024])

    A.seek(YBG_OFF)
    ybg = A.alloc([128, KC, T], BF16)
    assert A.off == X2_OFF
    Et = A.alloc([128, 16, 384], BF16)
    absd = A.alloc([128, 384], F32)
    maskd = A.alloc([128, 384], F32)
    etmp = A.alloc([128, 384], F32)
    wq = A.alloc([128, KC, 256], BF16)
    wkk = A.alloc([128, KC, 128], BF16)
    wvv = A.alloc([128, KC, 128], BF16)
    wzb = A.alloc([128, KC, 256], BF16)
    qT = A.alloc([128, 2, T], BF16)
    kT = A.alloc([128, TH], BF16)
    vv = A.alloc([128, NT, 128], BF16)
    gB = A.alloc([128, 2, T], BF16)
    pT = [A.alloc([128, 4, 384], BF16) for _ in range(4)]
    ext = [A.alloc([128, 384], F32) for _ in range(3)]
    Rt = [A.alloc([128, 512], F32) for _ in range(2)]
    Tn = [A.alloc([128, 512], F32) for _ in range(2)]
    phase_sems()
    sem_wc = newsem('sw')

    P.add("pool", lambda e: e.iota(iota_i[:, :], pattern=[[1, 384]], base=-128, channel_multiplier=-1), w=["iota"])
    P.add("dve", lambda e: e.tensor_copy(out=absd[:], in_=iota_i[:]), r=["iota"], w=["absd"])
    P.add("dve", lambda e: e.tensor_scalar(out=maskd[:], in0=absd[:], scalar1=-1.0, scalar2=None, op0=ALU.mult), r=["absd"], w=["maskd"])
    P.add("dve", lambda e: e.tensor_tensor(out=absd[:], in0=absd[:], in1=maskd[:], op=ALU.max), r=["absd", "maskd"], w=["absd"])
    P.add("dve", lambda e: e.tensor_scalar(out=maskd[:], in0=absd[:], scalar1=128.0, scalar2=None, op0=ALU.is_le), r=["absd"], w=["maskd"])
    for h in range(16):
        slope = float(2.0 ** (-8.0 * (h + 1) / 16.0))
        P.add("act", lambda e, slope=slope: e.activation(out=etmp[:], in_=absd[:], func=AF.Exp, scale=-slope), r=["absd"], w=["etmp"])
        P.add("dve", lambda e, h=h: e.tensor_tensor(out=Et[:, h, :], in0=etmp[:], in1=maskd[:], op=ALU.mult), r=["etmp", "maskd"], w=[("Et", h)])

    exrot = [0]
    KB5 = ((0, 512), (512, 512), (1024, 512), (1536, 512), (2048, 256))

    def attn_group(g):
        pairs = [(wq[:], w_in_v[:, :, 2048 + g * 256:2048 + (g + 1) * 256]),
                 (wkk[:, :, 0:64], w_in_v[:, :, 3072 + g * 64:3072 + (g + 1) * 64]),
                 (wkk[:, :, 64:128], w_in_v[:, :, 3072 + g * 64:3072 + (g + 1) * 64]),
                 (wvv[:, :, 0:64], w_in_v[:, :, 3328 + g * 64:3328 + (g + 1) * 64]),
                 (wvv[:, :, 64:128], w_in_v[:, :, 3328 + g * 64:3328 + (g + 1) * 64]),
                 (wzb[:], w_in_v[:, :, 4608 + g * 256:4608 + (g + 1) * 256])]
        dma_multi("pool", pairs, sem_wc, w=["wC"])
        for j0, n in KB5:
            def evk(bk, j0=j0, n=n):
                P.add("dve", lambda e: e.tensor_copy(out=kT[:, j0:j0 + n], in_=ps[bk][:, 0:n]), r=[("ps", bk)], w=[("kT", j0)])
            proj_fm(lambda kc: wkk[:, kc, :], "wC", xnT, j0, n, evk)
        for i0 in range(0, NT, 4):
            nt = min(4, NT - i0)
            bk = nextbank()

            def fnv(e, i0=i0, nt=nt, bk=bk):
                for ii in range(nt):
                    for kc in range(KC):
                        ins = e.matmul(ps[bk][:, ii * 128:(ii + 1) * 128], lhsT=xnT[:, kc, (i0 + ii) * 128:(i0 + ii + 1) * 128], rhs=wvv[:, kc, :],
                                       start=(kc == 0), stop=(kc == KC - 1))
                return ins
            P.add("pe", fnv, r=["wC"], w=[("ps", bk)])
            P.add("act", lambda e, i0=i0, nt=nt, bk=bk: e.copy(out=vv[:, i0:i0 + nt, :], in_=ps[bk][:, 0:nt * 128].rearrange("p (a b) -> p a b", a=nt)),
                  r=[("ps", bk)], w=[("vv", i0)])
        for cq in range(2):
            for tb in range(4):
                def evq(bk, cq=cq, tb=tb):
                    P.add("act", lambda e: e.mul(out=qT[:, cq, tb * 512:(tb + 1) * 512], in_=ps[bk][:, :], mul=0.125), r=[("ps", bk)], w=[("qT", cq, tb)])
                proj_fm(lambda kc, cq=cq: wq[:, kc, cq * 128:(cq + 1) * 128], "wC", xnT, 128 + tb * 512, 512, evq)
        for cq in range(2):
            bc = BG + 8 + 2 * g + cq
            bza = vecs[:, bc:bc + 1]
            for tb in range(4):
                def evzb(bk, cq=cq, tb=tb, bza=bza):
                    P.add("act", lambda e: e.activation(out=gB[:, cq, tb * 512:(tb + 1) * 512], in_=ps[bk][:, :], func=AF.Sigmoid, bias=bza),
                          r=[("ps", bk), "vecs"], w=[("gB", cq, tb)])
                proj_fm(lambda kc, cq=cq: wzb[:, kc, cq * 128:(cq + 1) * 128], "wC", xnT, 128 + tb * 512, 512, evzb)
        qkeys = [("qT", cq, tb) for cq in range(2) for tb in range(4)]
        kkeys = [("kT", j0) for (j0, n) in KB5]
        vkeys = [("vv", i0) for i0 in range(0, NT, 4)]
        gkeys = [("gB", cq, tb) for cq in range(2) for tb in range(4)]

        def scores(kb):
            qb_lo = max(kb - 1, 1)
            qb_hi = min(kb + 1, 16)
            qlo = (qb_lo - 1) * 128
            n = (qb_hi - qb_lo + 1) * 128
            c0 = (qb_lo - (kb - 1)) * 128
            psl = kb % 4
            for cq in range(2):
                for half in range(2):
                    h = 4 * g + 2 * cq + half
                    slot = cq + 2 * half
                    bk = nextbank(0, 4)
                    r0 = half * 64
                    P.add("pe", lambda e, bk=bk, r0=r0, cq=cq: e.matmul(ps[bk][:, 0:n], lhsT=kT[r0:r0 + 64, kb * 128:(kb + 1) * 128], rhs=qT[r0:r0 + 64, cq, qlo:qlo + n],
                                                                        start=True, stop=True),
                          r=qkeys + kkeys, w=[("ps", bk)])
                    xi = exrot[0] % 3
                    exrot[0] += 1
                    P.add("act", lambda e, bk=bk, xi=xi: e.activation(out=ext[xi][:, 0:n], in_=ps[bk][:, 0:n], func=AF.Exp), r=[("ps", bk)], w=[("ext", xi)])
                    P.add("pool" if half == 1 else "dve", lambda e, xi=xi, slot=slot, h=h: e.tensor_tensor(out=pT[psl][:, slot, c0:c0 + n], in0=ext[xi][:, 0:n], in1=Et[:, h, c0:c0 + n], op=ALU.mult),
                          r=[("ext", xi), ("Et", h)], w=[("pT", psl, slot)])

        def pv(qb):
            bx = 4 + 2 * (qb % 2)
            by = bx + 1

            def fnx(e):
                for j, kb in enumerate((qb - 1, qb, qb + 1)):
                    dj = qb - kb + 1
                    ins = e.matmul(ps[bx][:, :].rearrange("p (a b) -> p a b", a=4), lhsT=vv[:, kb, :], rhs=pT[kb % 4][:, :, dj * 128:(dj + 1) * 128],
                                   start=(j == 0), stop=(j == 2))
                return ins

            def fny(e):
                for j, kb in enumerate((qb - 1, qb, qb + 1)):
                    dj = qb - kb + 1
                    lh = onesLR[:, 0, :] if kb == 0 else (onesLR[:, 1, :] if kb == NT - 1 else ones[:])
                    e.matmul(ps[by][:, :].rearrange("p (a b) -> p a b", a=4), lhsT=lh, rhs=pT[kb % 4][:, :, dj * 128:(dj + 1) * 128], start=(j == 0), stop=False)
                return e.matmul(ps[by][:, :], lhsT=ones[0:1, :], rhs=es_row[0:1, g, :, :].rearrange("p a b -> p (a b)"), start=False, stop=True)
            pkeys = [("pT", kb % 4, s) for kb in (qb - 1, qb, qb + 1) for s in range(4)]
            P.add("pe", fnx, r=pkeys + vkeys, w=[("ps", bx)])
            P.add("pe", fny, r=pkeys + ["ones", "onesLR", "es_row"], w=[("ps", by)])
            ri = qb % 2
            P.add("act", lambda e: e.activation(out=Rt[ri][:], in_=ps[by][:, :], func=AF.Ln), r=[("ps", by)], w=[("Rt", ri)])
            P.add("act", lambda e: e.activation(out=Rt[ri][:], in_=Rt[ri][:], func=AF.Exp, scale=-1.0), r=[("Rt", ri)], w=[("Rt", ri)])
            P.add("dve", lambda e: e.tensor_tensor(out=Tn[ri][:], in0=ps[bx][:, :], in1=Rt[ri][:], op=ALU.mult), r=[("ps", bx), ("Rt", ri)], w=[("Tn", ri)])
            q0 = (qb - 1) * 128
            for half in range(2):
                r0 = half * 64
                P.add("dve", lambda e, r0=r0, half=half: e.tensor_tensor(
                    out=ybg[r0:r0 + 64, 2 * g:2 * g + 2, q0:q0 + 128],
                    in0=Tn[ri][r0:r0 + 64, half * 256:(half + 1) * 256].rearrange("p (a b) -> p a b", a=2),
                    in1=gB[r0:r0 + 64, :, q0:q0 + 128], op=ALU.mult),
                    r=[("Tn", ri)] + gkeys, w=[("ybg", g, qb, half)])

        scores(0)
        scores(1)
        for kb in range(2, NT):
            scores(kb)
            if kb >= 3:
                pv(kb - 2)
        pv(NT - 3)
        pv(NT - 2)

    if stop == 'E':
        return finish_early(Rt[0][:].bitcast(F32) if False else absd[:, 0:384])
    for g in range(4):
        attn_group(g)
        if stop == 'C1':
            break
    P.barrier()
    if stop in ('C', 'C1'):
        return finish_early(Tn[0][:, :])

    A.seek(X2_OFF)
    x2 = A.alloc([128, 16, D], F32)
    assert A.off == X2_END
    t1 = [A.alloc([128, 512], F32) for _ in range(2)]
    xr = [A.alloc([128, D], F32) for _ in range(2)]
    A.seek(XNT_OFF, YBG_OFF)
    wo = A.alloc([128, KC, D], BF16)
    mT = [A.alloc([128, KC, 512], BF16) for _ in range(2)]
    phase_sems()
    sem_wo = newsem('sw')
    sem_xr = [newsem() for _ in range(2)]
    w_out_v = w_out.rearrange("(kc p) n -> p kc n", p=128)
    dma_multi("pool", [(wo[:, 0:4, :], w_out_v[:, 0:4, :]), (wo[:, 4:8, :], w_out_v[:, 4:8, :])], sem_wo, w=["wo"])
    t1rot = [0]

    def d1_group(tg):
        tc0 = tg * 512
        ms = tg % 2
        for c in range(KC):
            P.add("dve", lambda e, c=c: e.tensor_tensor(out=mT[ms][:, c, :], in0=Pa[:, c, tc0:tc0 + 512], in1=ybg[:, c, tc0:tc0 + 512], op=ALU.add),
                  r=[("Pa", c)], w=[("mT", ms, c)])
        if stop in ('D1a', 'D1q'):
            return
        for tl in range(4):
            tile = tg * 4 + tl
            xs_ = tile % 2
            dma("sp", xr[xs_][:], x_ext[128 + tile * 128:128 + (tile + 1) * 128, :], sem_xr[xs_], w=[("xr", xs_)])
            for nh in range(2):
                bk = nextbank()

                def fno(e, bk=bk, tl=tl, nh=nh):
                    for c in range(KC):
                        ins = e.matmul(ps[bk][:, :], lhsT=mT[ms][:, c, tl * 128:(tl + 1) * 128], rhs=wo[:, c, nh * 512:(nh + 1) * 512], start=(c == 0), stop=(c == KC - 1))
                    return ins
                P.add("pe", fno, r=[("mT", ms, c) for c in range(KC)] + ["wo"], w=[("ps", bk)])
                P.add("dve", lambda e, bk=bk, tile=tile, nh=nh, xs_=xs_: e.tensor_tensor(out=x2[:, tile, nh * 512:(nh + 1) * 512], in0=ps[bk][:, :], in1=xr[xs_][:, nh * 512:(nh + 1) * 512], op=ALU.add),
                      r=[("ps", bk), ("xr", xs_)], w=[("x2", tile, nh)])

    for tg in range(4):
        d1_group(tg)
        if stop == 'D1a':
            return finish_early(t1[0][:])
    P.barrier()
    if stop == 'D1':
        return finish_early(x2[:, 0, :])

    A.seek(C_END, X2_OFF)
    w2 = A.alloc([128, FC, D], BF16)
    hT = A.alloc([128, FC, 1024], BF16)
    wf = [A.alloc([128, KC, 2, 128], BF16) for _ in range(2)]
    xs2 = A.alloc([128, D], BF16)
    sqj2 = A.alloc([128, D], BF16)
    A.seek(X2_END)
    xn2T = A.alloc([128, KC, 1024], BF16)
    sg = [A.alloc([128, 512], F32) for _ in range(2)]
    ot = A.alloc([128, D], F32)
    gfin = A.alloc([128, D], F32)
    phase_sems()
    sem_w2 = newsem('sw')
    sem_wf = [newsem('sw') for _ in range(2)]
    sem_o = newsem()
    sem_g = newsem()
    w_f2_v = w_f2.rearrange("(j p) n -> p j n", p=128)
    dma_multi("pool", [(w2[:, 0:11, :], w_f2_v[:, 0:11, :]), (w2[:, 11:22, :], w_f2_v[:, 11:22, :])], sem_w2, w=["w2"])
    dma("sp", gfin[:], gfin_d.partition_broadcast(128), sem_g, w=["gfin"])
    w_f1_v = w_f1.rearrange("(kc p) n -> p kc n", p=128)
    finals = []
    wfrot = [0]
    sgrot = [0]

    def ffn_in(j, ws):
        dma_multi("pool", [(wf[ws][:, :, 0, :], w_f1_v[:, :, j * 128:(j + 1) * 128]),
                           (wf[ws][:, :, 1, :], w_f1_v[:, :, 2816 + j * 128:2816 + (j + 1) * 128])], sem_wf[ws], w=[("wf", ws)])
        xkeys = [("xn2T", tl * 128, hb) for tl in range(8) for hb in range(2)]
        for tgl in range(2):
            bg = nextbank()
            bu = nextbank()

            def fng(e, bk=bg, which=0, tgl=tgl):
                for kc in range(KC):
                    ins = e.matmul(ps[bk][:, :], lhsT=wf[ws][:, kc, which, :], rhs=xn2T[:, kc, tgl * 512:(tgl + 1) * 512], start=(kc == 0), stop=(kc == KC - 1))
                return ins

            def fnu(e, bk=bu, which=1, tgl=tgl):
                for kc in range(KC):
                    ins = e.matmul(ps[bk][:, :], lhsT=wf[ws][:, kc, which, :], rhs=xn2T[:, kc, tgl * 512:(tgl + 1) * 512], start=(kc == 0), stop=(kc == KC - 1))
                return ins
            P.add("pe", fng, r=[("wf", ws)] + xkeys, w=[("ps", bg)])
            P.add("pe", fnu, r=[("wf", ws)] + xkeys, w=[("ps", bu)])
            si = sgrot[0] % 2
            sgrot[0] += 1
            P.add("act", lambda e, bg=bg, si=si: e.activation(out=sg[si][:], in_=ps[bg][:, :], func=AF.Silu), r=[("ps", bg)], w=[("sg", si)])
            P.add("dve", lambda e, bu=bu, si=si, tgl=tgl: e.tensor_tensor(out=hT[:, j, tgl * 512:(tgl + 1) * 512], in0=sg[si][:], in1=ps[bu][:, :], op=ALU.mult),
                  r=[("sg", si), ("ps", bu)], w=[("hT", j, tgl)])

    def ffn_out(tile, tl):
        sscol = 40 + tile
        hkeys = [("hT", j, tl // 4) for j in range(FC)]
        for nh in range(2):
            bk = nextbank()

            def fnf(e, bk=bk, nh=nh):
                for j in range(FC):
                    ins = e.matmul(ps[bk][:, :], lhsT=hT[:, j, tl * 128:(tl + 1) * 128], rhs=w2[:, j, nh * 512:(nh + 1) * 512], start=(j == 0), stop=(j == FC - 1))
                return ins
            P.add("pe", fnf, r=hkeys + ["w2"], w=[("ps", bk)])
            P.add("dve", lambda e, bk=bk, nh=nh: e.tensor_tensor(out=x2[:, tile, nh * 512:(nh + 1) * 512], in0=ps[bk][:, :], in1=x2[:, tile, nh * 512:(nh + 1) * 512], op=ALU.add),
                  r=[("ps", bk), ("x2", tile, nh)], w=[("x2", tile, nh)])
        xk = [("x2", tile, 0), ("x2", tile, 1)]
        P.add("act", lambda e: e.activation(out=sqj2[:], in_=x2[:, tile, :], func=AF.Square, accum_out=ss[:, sscol:sscol + 1]),
              r=xk + ["ss"], w=[("ssc", sscol), "sqj2"])
        P.add("act", lambda e: e.activation(out=rt[:, sscol:sscol + 1], in_=ss[:, sscol:sscol + 1], func=AF.Sqrt, scale=1.0 / D, bias=EPS), r=[("ssc", sscol)], w=[("rt", sscol)])
        P.add("dve", lambda e: e.reciprocal(out=rstd[:, sscol:sscol + 1], in_=rt[:, sscol:sscol + 1]), r=[("rt", sscol)], w=[("rstd", sscol)])
        P.add("dve", lambda e: e.scalar_tensor_tensor(out=ot[:], in0=x2[:, tile, :], scalar=rstd[:, sscol:sscol + 1], in1=gfin[:], op0=ALU.mult, op1=ALU.mult),
              r=xk + [("rstd", sscol), "gfin"], w=["ot"])
        finals.append(dma("sp", out_d[tile * 128:(tile + 1) * 128, :], ot[:], sem_o, r=["ot"], w=[("out", tile)]))

    for half in range(2):
        for tl in range(8):
            tile = half * 8 + tl
            norm_a(x2[:, tile, :], [("x2", tile, 0), ("x2", tile, 1)], 20 + tile, sqj2, "sqj2")
            norm_b(x2[:, tile, :], [("x2", tile, 0), ("x2", tile, 1)], 20 + tile, xs2, "xs2", G2C, xn2T, tl * 128, "xn2T")
        for j in range(FC):
            ws = wfrot[0] % 2
            wfrot[0] += 1
            ffn_in(j, ws)
        for tl in range(8):
            ffn_out(half * 8 + tl, tl)

    P.emit(st, final_waits=finals)
    st.close()
    nc.all_engine_barrier()
    for s_ in list(P.esem.values()) + SEMS:
        nc.gpsimd.sem_clear(s_)
    nc.all_engine_barrier()
    return nc


_NC = None


def _prep(inputs):
    f = lambda k: np.ascontiguousarray(np.asarray(inputs[k], dtype=np.float32))
    x = f("x")
    vecs = np.zeros((128, NV), np.float32)
    pc = lambda v: np.asarray(v, np.float32).reshape(8, 128).T
    vecs[:, 0:8] = pc(inputs["norm_mix_g"][0])
    for k in range(4):
        vecs[:, 8 + k * 8:16 + k * 8] = pc(inputs["conv_w"][0, k])
    vecs[:, 40:48] = pc(inputs["conv_b"][0])
    for d in range(2):
        vecs[:, 48 + d * 8:56 + d * 8] = pc(inputs["lru_lambda"][0, d])
        vecs[:, 64 + d * 8:72 + d * 8] = pc(inputs["lru_ba"][0, d])
        vecs[:, 80 + d * 8:88 + d * 8] = pc(inputs["lru_bx"][0, d])
    vecs[:, 96:112] = np.asarray(inputs["b_gate"][0], np.float32).reshape(16, 128).T
    vecs[:, 112:120] = pc(inputs["norm_ffn_g"][0])
    wbd = np.zeros((4, 128, 8, 128), np.float32)
    for d in range(2):
        for gi, nm in enumerate(("lru_wa", "lru_wx")):
            W = np.asarray(inputs[nm][0, d], np.float32)
            for c in range(8):
                wbd[d * 2 + gi, 0:64, c, 0:64] = W[2 * c]
                wbd[d * 2 + gi, 64:128, c, 64:128] = W[2 * c + 1]
    sink = np.asarray(inputs["attn_sink"][0], np.float32)
    order = []
    for g in range(4):
        order += [4 * g, 4 * g + 2, 4 * g + 1, 4 * g + 3]
    sink_r = sink[order].reshape(1, 16)
    common = {"w_in": f("w_in")[0], "w_out": f("w_out")[0], "w_f1": f("w_ffn_in")[0], "w_f2": f("w_ffn_out")[0],
              "wbd": wbd, "vecs": vecs, "gfin": f("norm_final_g").reshape(1, D), "sink": np.ascontiguousarray(sink_r)}
    maps = []
    S = x.shape[1]
    cw = np.asarray(inputs["conv_w"][0], np.float32)
    zero_c = np.zeros((1024,), np.float32)
    taps = {0: [cw[0], cw[1], cw[2], cw[3], zero_c],
            1: [cw[3], cw[2], cw[1], cw[0], zero_c]}
    wbd_dir = {d: (wbd[d * 2 + 0], wbd[d * 2 + 1]) for d in range(2)}
    for core in range(8):
        b, q = core // 4, core % 4
        xe = np.zeros((TH, D), np.float32)
        lo = q * T - 128
        hi = (q + 1) * T + 128
        slo, shi = max(lo, 0), min(hi, S)
        xe[slo - lo:shi - lo] = x[b, slo:shi]
        fl = np.zeros((128, NFLAG), np.float32)
        fl[:, 0] = 1.0 if q > 0 else 0.0
        fl[:, 1] = 1.0 if q < 3 else 0.0
        xsl = np.zeros((NS, SR, D), np.float32)
        sv = np.zeros((128, NS * SV), np.float32)
        swbd = np.zeros((NS * 2, 128, 8, 128), np.float32)
        for s_i in range(NS):
            if s_i < q:
                d, chunk, chain = 0, s_i, (1.0 if s_i > 0 else 0.0)
                tok = np.arange(chunk * T - 2, (chunk + 1) * T + 2)
            else:
                j = s_i - q
                d, chunk, chain = 1, 3 - j, (1.0 if j > 0 else 0.0)
                tok = np.arange((chunk + 1) * T, (chunk + 1) * T - (T + 4), -1)
            ok = (tok >= 0) & (tok < S)
            rows = np.zeros((T + 4, D), np.float32)
            rows[ok] = x[b, tok[ok]]
            xsl[s_i, 0:T + 4] = rows
            fl[:, 2 + s_i] = chain
            vb = s_i * SV
            for k in range(5):
                sv[:, vb + k * 8:vb + k * 8 + 8] = pc(taps[d][k])
            sv[:, vb + 40:vb + 48] = pc(inputs["conv_b"][0])
            sv[:, vb + 48:vb + 56] = pc(inputs["lru_lambda"][0, d])
            sv[:, vb + 56:vb + 64] = pc(inputs["lru_ba"][0, d])
            sv[:, vb + 64:vb + 72] = pc(inputs["lru_bx"][0, d])
            swbd[s_i * 2 + 0] = wbd_dir[d][0]
            swbd[s_i * 2 + 1] = wbd_dir[d][1]
        if q >= 1:
            fl[:, 5 + (q - 1)] = 1.0
        if q <= 2:
            fl[:, 8 + 2] = 1.0
        m = dict(common)
        m["x_ext"] = xe
        m["flags"] = fl
        m["xslot"] = xsl
        m["svecs"] = sv
        m["swbd"] = swbd
        maps.append(m)
    return maps


def kernel(**inputs):
    global _NC
    if _NC is None:
        _NC = build_program()
    maps = _prep(inputs)
    res = run_bass_kernel_spmd(_NC, maps, core_ids=list(range(8)))
    out = np.zeros((2, 4 * T, D), np.float32)
    for core in range(8):
        b, q = core // 4, core % 4
        out[b, q * T:(q + 1) * T] = np.asarray(res.results[core]["out"], dtype=np.float32)
    return out
```
